# Optimizing a Trainium2 kernel written in Bass

```python
import jax, jax.numpy as jnp
from jax import lax
import numpy as np

D_MODEL = 1024
BATCH = 4
SEQ = 8192
DEPTH = 4

WINDOWS = (128, 512, 2048)
DILATIONS = (1, 4, 16)
N_GROUPS = 3
HEADS_PER_GROUP = 8
HEAD_DIM = 128
N_ATTN_HEADS = N_GROUPS * HEADS_PER_GROUP
ATTN_OUT_WIDTH = HEADS_PER_GROUP * HEAD_DIM
QKV_WIDTH = N_GROUPS * 3 * HEADS_PER_GROUP * HEAD_DIM
NUM_BUCKETS = 32
MAX_DISTANCE = 2048
LRU_WIDTH = D_MODEL
LRU_BLOCKS = 4
LRU_BLOCK_WIDTH = LRU_WIDTH // LRU_BLOCKS
LRU_CONV_WIDTH = 4
LRU_C = 8.0
D_FF = 3 * D_MODEL
FFN_CONV_WIDTH = 3

RMS_EPS = 1e-6
NEG_INF = -1e30
N_ATTN_LAYERS = (DEPTH + 1) // 2
N_LRU_LAYERS = DEPTH // 2

kernel_name = "hybrid_dilated_attn_rglru_convffn"


def rms_norm(x, g):
    xf = x.astype(jnp.float32)
    y = xf * lax.rsqrt(jnp.mean(xf * xf, axis=-1, keepdims=True) + RMS_EPS)
    return (y * g.astype(jnp.float32)).astype(x.dtype)


def causal_dwconv(x, w, b):
    K = w.shape[0]
    S = x.shape[1]
    xp = jnp.pad(x, ((0, 0), (K - 1, 0), (0, 0)))
    out = b
    for k in range(K):
        out = out + xp[:, k:k + S] * w[k]
    return out


def t5_bucket(dist):
    max_exact = NUM_BUCKETS // 2
    d = np.maximum(dist, 1).astype(np.float64)
    large = max_exact + (np.log(d / max_exact) / np.log(MAX_DISTANCE / max_exact)
                         * (NUM_BUCKETS - max_exact)).astype(np.int32)
    large = np.minimum(large, NUM_BUCKETS - 1)
    return np.where(dist < max_exact, dist, large).astype(np.int32)


def band_geometry(band):
    i = np.arange(band)[:, None]
    k = np.arange(2 * band)[None, :]
    m = i + band - k
    valid = (m >= 0) & (m <= band)
    return m, valid, k


def dilated_group_attention(q, k, v, bias_table, window, dilation):
    B, S, H, Dh = q.shape
    band = window // dilation
    span = dilation * band
    S_pad = -(-S // span) * span
    U = S_pad // dilation
    nb = U // band

    def to_strided(t):
        t = jnp.pad(t, ((0, 0), (0, S_pad - S), (0, 0), (0, 0)))
        t = t.reshape(B, U, dilation, H, Dh).transpose(0, 2, 3, 1, 4)
        return t.reshape(B, dilation, H, nb, band, Dh)

    qs, ks, vs = to_strided(q), to_strided(k), to_strided(v)
    pad_blk = ((0, 0), (0, 0), (0, 0), (1, 0), (0, 0), (0, 0))
    kk = jnp.concatenate([jnp.pad(ks, pad_blk)[:, :, :, :-1], ks], axis=-2)
    vv = jnp.concatenate([jnp.pad(vs, pad_blk)[:, :, :, :-1], vs], axis=-2)

    m, valid, kidx = band_geometry(band)
    bucket = t5_bucket(np.clip(m, 0, band) * dilation)
    bias = jnp.transpose(bias_table.astype(jnp.float32)[bucket], (2, 0, 1))
    blk = np.arange(nb)[:, None, None]
    mask = valid[None] & ((blk > 0) | (kidx[None] >= band))

    scale = HEAD_DIM ** -0.5
    logits = jnp.einsum('bxhnqd,bxhnkd->bxhnqk', qs, kk,
                        preferred_element_type=jnp.float32) * scale
    logits = logits + bias[None, None, :, None]
    logits = jnp.where(mask[None, None, None], logits, NEG_INF)
    mx = jnp.max(logits, axis=-1, keepdims=True)
    p = jnp.exp(logits - mx)
    s = jnp.sum(p, axis=-1, keepdims=True)
    o = jnp.einsum('bxhnqk,bxhnkd->bxhnqd', p, vv.astype(jnp.float32)) / s
    lse = (mx + jnp.log(s))[..., 0]

    o = o.reshape(B, dilation, H, U, Dh).transpose(0, 3, 1, 2, 4).reshape(B, S_pad, H, Dh)[:, :S]
    lse = lse.reshape(B, dilation, H, U).transpose(0, 3, 1, 2).reshape(B, S_pad, H)[:, :S]
    return o, lse


def dilated_attention_mixer(x, w_qkv, w_o, rel_bias):
    B, S, _ = x.shape
    qkv = (x @ w_qkv).reshape(B, S, N_GROUPS, 3, HEADS_PER_GROUP, HEAD_DIM)
    outs, lses = [], []
    for g in range(N_GROUPS):
        tbl = rel_bias[:, g * HEADS_PER_GROUP:(g + 1) * HEADS_PER_GROUP]
        o, lse = dilated_group_attention(qkv[:, :, g, 0], qkv[:, :, g, 1], qkv[:, :, g, 2],
                                         tbl, WINDOWS[g], DILATIONS[g])
        outs.append(o)
        lses.append(lse)
    alpha = jax.nn.softmax(jnp.stack(lses), axis=0)
    o = jnp.sum(alpha[..., None] * jnp.stack(outs), axis=0)
    return o.reshape(B, S, ATTN_OUT_WIDTH).astype(x.dtype) @ w_o


def _lru_combine(c1, c2):
    a1, b1 = c1
    a2, b2 = c2
    return a1 * a2, a2 * b1 + b2


def rglru_mixer(x, w_in, conv_w, conv_b, ga_w, ga_b, gx_w, gx_b, lam, w_out):
    B, S, _ = x.shape
    h = x @ w_in
    xb, gb = h[..., :LRU_WIDTH], h[..., LRU_WIDTH:]
    gate = jax.nn.gelu(gb, approximate=True)
    xb = causal_dwconv(xb, conv_w, conv_b)
    xr = xb.reshape(B, S, LRU_BLOCKS, LRU_BLOCK_WIDTH)
    r = jax.nn.sigmoid(jnp.einsum('bsnc,ncd->bsnd', xr, ga_w) + ga_b).reshape(B, S, LRU_WIDTH)
    i = jax.nn.sigmoid(jnp.einsum('bsnc,ncd->bsnd', xr, gx_w) + gx_b).reshape(B, S, LRU_WIDTH)
    log_a = LRU_C * r.astype(jnp.float32) * jax.nn.log_sigmoid(lam.astype(jnp.float32))
    a = jnp.exp(log_a)
    mult = jnp.sqrt(jnp.maximum(-jnp.expm1(2.0 * log_a), 0.0))
    b = mult * i.astype(jnp.float32) * xb.astype(jnp.float32)
    _, hs = lax.associative_scan(_lru_combine, (a, b), axis=1)
    return (hs.astype(x.dtype) * gate) @ w_out


def conv_ffn(x, w_up, conv_w, conv_b, w_down):
    u = causal_dwconv(x @ w_up, conv_w, conv_b)
    g, v = u[..., :D_FF], u[..., D_FF:]
    return (jax.nn.gelu(g, approximate=True) * v) @ w_down


def setup_inputs(seed: int = 0) -> dict:
    key = jax.random.key(seed)
    ks = jax.random.split(key, 32)
    f32 = jnp.float32
    nrm = lambda k, shape, s: jax.random.normal(k, shape, f32) * s
    gain = lambda k: 1.0 + 0.05 * jax.random.normal(k, (DEPTH, D_MODEL), f32)
    a_target = jax.random.uniform(ks[20], (N_LRU_LAYERS, LRU_WIDTH), f32, 0.9, 0.999)
    p = a_target ** (1.0 / LRU_C)
    lam = jnp.log(p) - jnp.log1p(-p)
    return {
        "x": nrm(ks[0], (BATCH, SEQ, D_MODEL), 1.0),
        "norm_mix_pre": gain(ks[1]),
        "norm_mix_post": gain(ks[2]),
        "norm_ffn_pre": gain(ks[3]),
        "norm_ffn_post": gain(ks[4]),
        "rel_bias": nrm(ks[5], (NUM_BUCKETS, N_ATTN_HEADS), 0.5),
        "attn_w_qkv": nrm(ks[6], (N_ATTN_LAYERS, D_MODEL, QKV_WIDTH), D_MODEL ** -0.5),
        "attn_w_o": nrm(ks[7], (N_ATTN_LAYERS, ATTN_OUT_WIDTH, D_MODEL), ATTN_OUT_WIDTH ** -0.5),
        "lru_w_in": nrm(ks[8], (N_LRU_LAYERS, D_MODEL, 2 * LRU_WIDTH), D_MODEL ** -0.5),
        "lru_conv_w": nrm(ks[9], (N_LRU_LAYERS, LRU_CONV_WIDTH, LRU_WIDTH), LRU_CONV_WIDTH ** -0.5),
        "lru_conv_b": nrm(ks[10], (N_LRU_LAYERS, LRU_WIDTH), 0.02),
        "lru_ga_w": nrm(ks[11], (N_LRU_LAYERS, LRU_BLOCKS, LRU_BLOCK_WIDTH, LRU_BLOCK_WIDTH), LRU_BLOCK_WIDTH ** -0.5),
        "lru_ga_b": nrm(ks[12], (N_LRU_LAYERS, LRU_BLOCKS, LRU_BLOCK_WIDTH), 0.02),
        "lru_gx_w": nrm(ks[13], (N_LRU_LAYERS, LRU_BLOCKS, LRU_BLOCK_WIDTH, LRU_BLOCK_WIDTH), LRU_BLOCK_WIDTH ** -0.5),
        "lru_gx_b": nrm(ks[14], (N_LRU_LAYERS, LRU_BLOCKS, LRU_BLOCK_WIDTH), 0.02),
        "lru_lambda": lam,
        "lru_w_out": nrm(ks[15], (N_LRU_LAYERS, LRU_WIDTH, D_MODEL), LRU_WIDTH ** -0.5),
        "ffn_w_up": nrm(ks[16], (DEPTH, D_MODEL, 2 * D_FF), D_MODEL ** -0.5),
        "ffn_conv_w": nrm(ks[17], (DEPTH, FFN_CONV_WIDTH, 2 * D_FF), FFN_CONV_WIDTH ** -0.5),
        "ffn_conv_b": nrm(ks[18], (DEPTH, 2 * D_FF), 0.02),
        "ffn_w_down": nrm(ks[19], (DEPTH, D_FF, D_MODEL), D_FF ** -0.5),
    }


def reference(x, norm_mix_pre, norm_mix_post, norm_ffn_pre, norm_ffn_post, rel_bias,
              attn_w_qkv, attn_w_o, lru_w_in, lru_conv_w, lru_conv_b, lru_ga_w, lru_ga_b,
              lru_gx_w, lru_gx_b, lru_lambda, lru_w_out, ffn_w_up, ffn_conv_w, ffn_conv_b,
              ffn_w_down):
    for layer in range(DEPTH):
        j = layer // 2
        h = rms_norm(x, norm_mix_pre[layer])
        if layer % 2 == 0:
            h = dilated_attention_mixer(h, attn_w_qkv[j], attn_w_o[j], rel_bias)
        else:
            h = rglru_mixer(h, lru_w_in[j], lru_conv_w[j], lru_conv_b[j], lru_ga_w[j], lru_ga_b[j],
                            lru_gx_w[j], lru_gx_b[j], lru_lambda[j], lru_w_out[j])
        x = x + rms_norm(h, norm_mix_post[layer])
        h = rms_norm(x, norm_ffn_pre[layer])
        h = conv_ffn(h, ffn_w_up[layer], ffn_conv_w[layer], ffn_conv_b[layer], ffn_w_down[layer])
        x = x + rms_norm(h, norm_ffn_post[layer])
    return x
```

```python
import numpy as np
import os
DBGD = os.environ.get('DBGD', '')
from contextlib import ExitStack
import concourse.bass as bass
import concourse.mybir as mybir
from concourse.bass_utils import run_bass_kernel_spmd

F32, BF16 = mybir.dt.float32, mybir.dt.bfloat16
AF = mybir.ActivationFunctionType
ALU = mybir.AluOpType

D = 1024
DEPTH = 4
DIL = (1, 4, 16)
DFF = 3072
ST = 4096
TT = 512
EPS = 1e-6
SCALE = 128 ** -0.5
N_CORES = 8

def _cst_layout():
    off = {}
    n = 0
    def add(name, k):
        nonlocal n
        off[name] = n
        n += k
    for kind in ("mix_pre", "mix_post", "ffn_pre", "ffn_post"):
        for l in range(DEPTH):
            add((kind, l), 8)
    for j in range(2):
        add(("lru_cw", j), 4 * 8)
        add(("lru_cb", j), 8)
        add(("lru_gab", j), 8)
        add(("lru_gxb", j), 8)
        add(("lru_lam", j), 8)
    for l in range(DEPTH):
        add(("ffn_cw", l), 3 * 48)
        add(("ffn_cb", l), 48)
    return off, n

CST_OFF, NCST = _cst_layout()


def _pm(v):
    return np.ascontiguousarray(np.asarray(v, np.float32).reshape(-1, 128).T)


def build_consts(inp):
    c = np.zeros((128, NCST), np.float32)
    names = {"mix_pre": "norm_mix_pre", "mix_post": "norm_mix_post",
             "ffn_pre": "norm_ffn_pre", "ffn_post": "norm_ffn_post"}
    for kind, nm in names.items():
        for l in range(DEPTH):
            o = CST_OFF[(kind, l)]
            c[:, o:o + 8] = _pm(inp[nm][l])
    for j in range(2):
        o = CST_OFF[("lru_cw", j)]
        for t in range(4):
            c[:, o + t * 8:o + t * 8 + 8] = _pm(inp["lru_conv_w"][j, t])
        c[:, CST_OFF[("lru_cb", j)]:CST_OFF[("lru_cb", j)] + 8] = _pm(inp["lru_conv_b"][j])
        c[:, CST_OFF[("lru_gab", j)]:CST_OFF[("lru_gab", j)] + 8] = _pm(inp["lru_ga_b"][j].reshape(-1))
        c[:, CST_OFF[("lru_gxb", j)]:CST_OFF[("lru_gxb", j)] + 8] = _pm(inp["lru_gx_b"][j].reshape(-1))
        c[:, CST_OFF[("lru_lam", j)]:CST_OFF[("lru_lam", j)] + 8] = _pm(inp["lru_lambda"][j])
    for l in range(DEPTH):
        o = CST_OFF[("ffn_cw", l)]
        for t in range(3):
            c[:, o + t * 48:o + t * 48 + 48] = _pm(inp["ffn_conv_w"][l, t])
        o = CST_OFF[("ffn_cb", l)]
        c[:, o:o + 48] = _pm(inp["ffn_conv_b"][l])
    return c


def _t5_bucket(dist):
    max_exact = 16
    d = np.maximum(dist, 1).astype(np.float64)
    large = max_exact + (np.log(d / max_exact) / np.log(2048 / max_exact) * (32 - max_exact)).astype(np.int32)
    large = np.minimum(large, 31)
    return np.where(dist < max_exact, dist, large).astype(np.int32)


def build_biasmask(rel_bias):
    rel_bias = np.asarray(rel_bias, np.float32)
    bm = np.full((128, 3, 8, 256), -1e30, np.float32)
    kk = np.arange(128)[:, None]
    ii = np.arange(128)[None, :]
    for g in range(3):
        d = DIL[g]
        m_same = ii - kk
        m_next = ii + 128 - kk
        for hd in range(8):
            tbl = rel_bias[:, g * 8 + hd]
            b_same = tbl[_t5_bucket(np.clip(m_same, 0, 128) * d)]
            b_next = tbl[_t5_bucket(np.clip(m_next, 0, 128) * d)]
            bm[:, g, hd, 0:128] = np.where(m_same >= 0, b_same, np.float32(-1e30))
            bm[:, g, hd, 128:256] = np.where(m_next <= 128, b_next, np.float32(-1e30))
    return np.ascontiguousarray(bm.reshape(128, 3 * 8 * 256))


ENGS = ("pe", "act", "dve", "pool", "sp")


class Sched:
    def __init__(self, nc, stack, n_dma=64):
        self.nc = nc
        self.esem = {e: stack.enter_context(nc.semaphore("se_" + e)) for e in ENGS}
        self.ecnt = {e: 0 for e in ENGS}
        self.dsem = [stack.enter_context(nc.semaphore("sd%d" % i)) for i in range(n_dma)]
        self.dcnt = [0] * n_dma
        self.nphase = 0


class Phase:
    def __init__(self, S):
        self.S = S
        self.ops = []
        self.lw = {}
        self.rd = {}
        self.keyidx = {}

    def add(self, eng, fn, reads=(), writes=(), dma_key=None):
        idx = len(self.ops)
        deps = set()
        for r in reads:
            w = self.lw.get(r)
            if w is not None:
                deps.add(w)
        for r in writes:
            w = self.lw.get(r)
            if w is not None:
                deps.add(w)
            rl = self.rd.get(r)
            if rl:
                deps.update(rl)
        for r in reads:
            self.rd.setdefault(r, []).append(idx)
        for r in writes:
            self.lw[r] = idx
            self.rd[r] = []
        deps.discard(idx)
        dk = None
        if dma_key is not None:
            if dma_key not in self.keyidx:
                self.keyidx[dma_key] = len(self.keyidx)
                assert len(self.keyidx) <= len(self.S.dsem), "too many dma keys"
            dk = self.keyidx[dma_key]
        self.ops.append([eng, fn, deps, dk, False, 0])
        return idx

    def mm(self, out, lhsT, rhs, start, stop, reads, writes):
        self.add("pe", lambda e: e.matmul(out, lhsT, rhs, start=start, stop=stop), reads, writes)

    def act(self, out, in_, func, reads, writes, scale=None, bias=None):
        kw = {}
        if scale is not None:
            kw["scale"] = scale
        if bias is not None:
            kw["bias"] = bias
        self.add("act", lambda e: e.activation(out=out, in_=in_, func=func, **kw), reads, writes)

    def ts(self, eng, out, in0, s1, s2, op0, op1, reads, writes):
        if op1 is None:
            self.add(eng, lambda e: e.tensor_scalar(out, in0, s1, None, op0), reads, writes)
        else:
            self.add(eng, lambda e: e.tensor_scalar(out, in0, s1, s2, op0, op1), reads, writes)

    def stt(self, out, in0, scalar, in1, op0, op1, reads, writes):
        self.add("dve", lambda e: e.scalar_tensor_tensor(out, in0, scalar, in1, op0, op1), reads, writes)

    def tt(self, eng, out, in0, in1, op, reads, writes):
        self.add(eng, lambda e: e.tensor_tensor(out, in0, in1, op), reads, writes)

    def cp(self, eng, out, in_, reads, writes):
        if eng == "act":
            self.add(eng, lambda e: e.copy(out, in_), reads, writes)
        else:
            self.add(eng, lambda e: e.tensor_copy(out, in_), reads, writes)

    def recip(self, out, in_, reads, writes):
        self.add("dve", lambda e: e.reciprocal(out, in_), reads, writes)

    def memset(self, eng, ap, val, writes):
        self.add(eng, lambda e: e.memset(ap, val), (), writes)

    def dma(self, out, in_, reads, writes, key, eng="sp"):
        self.add(eng, lambda e: e.dma_start(out=out, in_=in_), reads, writes, dma_key=key)

    def emit(self):
        S = self.S
        nc = S.nc
        ops = self.ops
        n = len(ops)
        dcount = {}
        dc = list(S.dcnt)
        for i, op in enumerate(ops):
            if op[3] is not None:
                dc[op[3]] += 1
                dcount[i] = dc[op[3]]
        need = [None] * n
        for i, op in enumerate(ops):
            eng = op[0]
            per_eng = {}
            per_dma = {}
            for y in op[2]:
                oy = ops[y]
                if oy[3] is not None:
                    k = oy[3]
                    if dcount[y] > per_dma.get(k, 0):
                        per_dma[k] = dcount[y]
                else:
                    if oy[0] == "pe" and eng == "pe":
                        continue
                    if y > per_eng.get(oy[0], -1):
                        per_eng[oy[0]] = y
            for y in per_eng.values():
                ops[y][4] = True
            need[i] = (per_eng, per_dma)
        ec = dict(S.ecnt)
        for op in ops:
            if op[3] is None and op[4]:
                ec[op[0]] += 1
                op[5] = ec[op[0]]
        per = {e: [] for e in ENGS}
        for i, op in enumerate(ops):
            per[op[0]].append(i)
        final_dma = [(k, dc[k]) for k in range(len(dc)) if dc[k] > S.dcnt[k]]

        def emit_eng(e, name):
            waited = {}
            for i in per[name]:
                op = ops[i]
                pe_, pd_ = need[i]
                for en, y in pe_.items():
                    v = ops[y][5]
                    key = ("e", en)
                    if waited.get(key, 0) < v:
                        e.wait_ge(S.esem[en], v)
                        waited[key] = v
                for k, c in pd_.items():
                    key = ("d", k)
                    if waited.get(key, 0) < c * 16:
                        e.wait_ge(S.dsem[k], c * 16)
                        waited[key] = c * 16
                ins = op[1](e)
                if op[3] is not None:
                    ins.then_inc(S.dsem[op[3]], 16)
                elif op[4]:
                    ins.then_inc(S.esem[name], 1)
            if name == "sp":
                for k, c in final_dma:
                    if waited.get(("d", k), 0) < c * 16:
                        e.wait_ge(S.dsem[k], c * 16)

        with nc.Block() as blk:
            @blk.tensor
            def _(e):
                emit_eng(e, "pe")

            @blk.scalar
            def _(e):
                emit_eng(e, "act")

            @blk.vector
            def _(e):
                emit_eng(e, "dve")

            @blk.gpsimd
            def _(e):
                emit_eng(e, "pool")

            @blk.sync
            def _(e):
                emit_eng(e, "sp")
        S.ecnt = ec
        S.dcnt = dc
        S.nphase += 1


class Builder:
    def __init__(self, nc, S_tok, plan):
        self.nc = nc
        self.S_tok = S_tok
        self.NT = S_tok // ST
        self.plan = plan
        self._n = 0

    def sb(self, stack, shape, dt, name=None):
        self._n += 1
        nm = name or ("t%d" % self._n)
        t = stack.enter_context(self.nc.sbuf_tensor(nm, shape, dt))
        return t

    def dram_in(self, name, shape, dt=F32):
        return self.nc.dram_tensor(name, list(shape), dt, kind="ExternalInput").ap()

    def build(self):
        nc = self.nc
        S_tok = self.S_tok
        self.x_in = self.dram_in("xT", [D, S_tok])
        self.y_out = nc.dram_tensor("yT", [D, S_tok], F32, kind="ExternalOutput").ap()
        self.cst_d = self.dram_in("cst", [128, NCST])
        self.bm_d = self.dram_in("bm", [128, 3 * 8 * 256])
        self.w_qkv = self.dram_in("attn_w_qkv", [2, D, 9216])
        self.w_o = self.dram_in("attn_w_o", [2, D, D])
        self.w_in = self.dram_in("lru_w_in", [2, D, 2 * D])
        self.w_ga = self.dram_in("lru_ga_w", [2, 4, 256, 256])
        self.w_gx = self.dram_in("lru_gx_w", [2, 4, 256, 256])
        self.w_out = self.dram_in("lru_w_out", [2, D, D])
        self.w_up = self.dram_in("ffn_w_up", [DEPTH, D, 2 * DFF])
        self.w_down = self.dram_in("ffn_w_down", [DEPTH, DFF, D])
        self.xs = nc.dram_tensor("xs", [D, S_tok], F32, kind="Internal").ap()
        self.zT = nc.dram_tensor("zT", [DFF, S_tok], BF16, kind="Internal").ap()
        self.QT = nc.dram_tensor("QT", [3, 8, 128, S_tok], BF16, kind="Internal").ap()
        self.KT = nc.dram_tensor("KT", [3, 8, 128, S_tok], BF16, kind="Internal").ap()
        self.V = nc.dram_tensor("Vs", [3, 8, 128, S_tok // 128, 128], BF16, kind="Internal").ap()

        with ExitStack() as top:
            self.S = Sched(nc, top)
            self.ps = [top.enter_context(nc.psum_tensor("ps%d" % i, [128, 512], F32)) for i in range(8)]
            self.cst = self.sb(top, [128, NCST], F32, "cst_sb")
            self.ones = self.sb(top, [128, 128], BF16, "ones")
            self.utail = self.sb(top, [128, 48, 2], F32, "utail")
            self.xbtail = self.sb(top, [128, 8, 3], F32, "xbtail")
            self.hstate = self.sb(top, [128, 8], F32, "hstate")
            self.ls8 = self.sb(top, [128, 16], F32, "ls8")
            self.epsb = self.sb(top, [128, 1], F32, "epsb")
            self.tinyb = self.sb(top, [128, 1], F32, "tinyb")
            self.phase_init()
            cur = self.x_in
            nsub = len(self.plan)
            for si, (kind, layer) in enumerate(self.plan):
                dst = self.y_out if si == nsub - 1 else self.xs
                for st in range(self.NT):
                    if kind == "attn":
                        self.sub_attn(layer, st, cur, dst)
                    elif kind == "lru":
                        self.sub_lru(layer, st, cur, dst)
                    elif kind == "ffn":
                        self.sub_ffn(layer, st, cur, dst)
                cur = dst
        return nc

    def ccol(self, key, i=0, n=1):
        o = CST_OFF[key] + i
        return self.cst[:, o:o + n]

    def phase_init(self):
        P = Phase(self.S)
        P.dma(self.cst[:], self.cst_d[:, :], (), ["cst"], "cst")
        P.memset("dve", self.ones[:], 1.0, ["ones"])
        P.memset("dve", self.epsb[:], EPS, ["epsb"])
        P.memset("dve", self.tinyb[:], 1e-30, ["tinyb"])
        for j in range(2):
            o = CST_OFF[("lru_lam", j)]
            dst = self.ls8[:, j * 8:(j + 1) * 8]
            P.act(dst, self.cst[:, o:o + 8], AF.Exp, ["cst"], [("ls8", j)], scale=-1.0)
            P.ts("dve", dst, dst, 1.0, None, ALU.add, None, [("ls8", j)], [("ls8", j)])
            P.act(dst, dst, AF.Ln, [("ls8", j)], [("ls8", j)])
            P.ts("dve", dst, dst, -8.0, None, ALU.mult, None, [("ls8", j)], [("ls8", j)])
        P.emit()

    def load_w(self, P, w_ap, stage, stage_key, dst_ap, dst_key):
        P.dma(stage, w_ap.rearrange("(kc p) c -> p kc c", p=128), (), [stage_key], stage_key)
        P.cp("pool", dst_ap, stage, [stage_key], [dst_key])

    def phase_norm(self, src, st, gkey, h):
        with ExitStack() as ph:
            xt = [self.sb(ph, [128, 8, TT], F32) for _ in range(2)]
            sq = [self.sb(ph, [128, 8, TT], BF16) for _ in range(2)]
            sd = [self.sb(ph, [128, TT], F32) for _ in range(2)]
            P = Phase(self.S)
            psn = [self.ps[0], self.ps[1]]
            for i in range(ST // TT):
                s = i % 2
                c0 = st * ST + i * TT
                kx, kq, kd, kp = ("nx", s), ("nq", s), ("nd", s), ("psn", s)
                P.dma(xt[s][:], src[:, c0:c0 + TT].rearrange("(kc p) t -> p kc t", p=128), (), [kx], kx)
                P.act(sq[s][:], xt[s][:], AF.Square, [kx], [kq])
                for kc in range(8):
                    P.mm(psn[s][:], self.ones[:], sq[s][:, kc, :], kc == 0, kc == 7, [kq, "ones"], [kp])
                P.act(sd[s][:], psn[s][:], AF.Sqrt, [kp], [kd], scale=1.0 / D, bias=self.epsb[:])
                P.recip(sd[s][:], sd[s][:], [kd], [kd])
                for kc in range(8):
                    P.stt(h[:, kc, i * TT:(i + 1) * TT], xt[s][:, kc, :], self.ccol(gkey, kc), sd[s][:],
                          ALU.mult, ALU.mult, [kx, kd, "cst"], [("h", i)])
            P.emit()

    def phase_down(self, z_d, KC, w_d, gkey, src, dst, st):
        with ExitStack() as ph:
            wres = self.sb(ph, [128, KC, D], BF16)
            wst = [self.sb(ph, [128, 4, D], F32) for _ in range(2)]
            zt = [self.sb(ph, [128, KC, TT], BF16) for _ in range(2)]
            xt = self.sb(ph, [128, 8, TT], F32)
            ot = self.sb(ph, [128, 8, TT], F32)
            sq = self.sb(ph, [128, 8, TT], BF16)
            sd = self.sb(ph, [128, TT], F32)
            P = Phase(self.S)
            for q in range(KC // 4):
                s = q % 2
                if 'w' in DBGD:
                    continue
                if 'W' in DBGD:
                    P.dma(wst[s][:], w_d[q * 512:(q + 1) * 512, :].rearrange("(kc p) c -> p kc c", p=128), (), [("wst", s)], ("wst", s))
                    continue
                self.load_w(P, w_d[q * 512:(q + 1) * 512, :], wst[s][:], ("wst", s),
                            wres[:, q * 4:(q + 1) * 4, :], ("wres", q))
            wkeys = [("wres", q) for q in range(KC // 4)]
            ntile = ST // TT

            def load_z(i):
                c0 = st * ST + i * TT
                for q8 in range(KC // 8):
                    P.dma(zt[i % 2][:, q8 * 8:(q8 + 1) * 8, :],
                          z_d[q8 * 1024:(q8 + 1) * 1024, c0:c0 + TT].rearrange("(kc p) t -> p kc t", p=128), (),
                          [("zt", i % 2, q8)], ("zt", i % 2, q8))
            if 'z' not in DBGD:
                load_z(0)
            for i in range(ntile):
                if i + 1 < ntile and 'z' not in DBGD:
                    load_z(i + 1)
                s = i % 2
                c0 = st * ST + i * TT
                P.dma(xt[:], src[:, c0:c0 + TT].rearrange("(kc p) t -> p kc t", p=128), (), ["dx"], "dx")
                lvl = int(os.environ.get('DBGL', '9'))
                for oc in range(8 if lvl >= 2 else 0):
                    pb = self.ps[2 + oc % 4]
                    kp = ("pso", oc % 4)
                    for kc in range(KC if 'm' not in DBGD else 1):
                        P.mm(pb[:], wres[:, kc, oc * 128:(oc + 1) * 128], zt[s][:, kc, :], kc == 0, kc == (KC - 1 if 'm' not in DBGD else 0),
                             [("zt", s, kc // 8)] + wkeys, [kp])
                    P.act(sq[:, oc, :], pb[:], AF.Square, [kp], [("dsq", oc)])
                    P.cp("dve", ot[:, oc, :], pb[:], [kp, ("dsq", oc)], [("dot", oc)])
                for kc in range(8 if lvl >= 3 else 0):
                    P.mm(self.ps[0][:], self.ones[:], sq[:, kc, :], kc == 0, kc == 7, [("dsq", kc), "ones"], ["psn"])
                if lvl >= 3:
                    P.act(sd[:], self.ps[0][:], AF.Sqrt, ["psn"], ["dsd"], scale=1.0 / D, bias=self.epsb[:])
                    P.recip(sd[:], sd[:], ["dsd"], ["dsd"])
                for oc in range(8 if lvl >= 4 else 0):
                    P.stt(ot[:, oc, :], ot[:, oc, :], self.ccol(gkey, oc), sd[:], ALU.mult, ALU.mult,
                          [("dot", oc), "dsd", "cst"], [("dot", oc)])
                    if lvl >= 5:
                        P.tt("pool", xt[:, oc, :], xt[:, oc, :], ot[:, oc, :], ALU.add, [("dot", oc), "dx"], [("dxo", oc)])
                P.dma(dst[:, c0:c0 + TT].rearrange("(kc p) t -> p kc t", p=128), xt[:],
                      [("dxo", oc) for oc in range(8)] + ["dx"], [], "dxs")
            P.emit()

    def sub_ffn(self, layer, st, src, dst):
        with ExitStack() as sub:
            h = self.sb(sub, [128, 8, ST], BF16, "h_ffn%d_%d" % (layer, st))
            self.phase_norm(src, st, ("ffn_pre", layer), h)
            self.phase_ffn_up(layer, st, h)
        self.phase_down(self.zT, 24, self.w_down[layer], ("ffn_post", layer), src, dst, st)

    def phase_ffn_up(self, layer, st, h):
        WB = 256
        with ExitStack() as ph:
            wst = [self.sb(ph, [128, 8, WB], F32) for _ in range(2)]
            wbf = [self.sb(ph, [128, 8, WB], BF16) for _ in range(4)]
            ug = [self.sb(ph, [128, TT + 2], F32) for _ in range(2)]
            uv = [self.sb(ph, [128, TT + 2], F32) for _ in range(2)]
            cg = [self.sb(ph, [128, TT], F32) for _ in range(2)]
            cv = [self.sb(ph, [128, TT], F32) for _ in range(2)]
            gl = [self.sb(ph, [128, TT], F32) for _ in range(2)]
            ctmp = [self.sb(ph, [128, TT], F32) for _ in range(2)]
            zst = [self.sb(ph, [128, ST], BF16) for _ in range(2)]
            P = Phase(self.S)
            if st == 0:
                P.memset("pool", self.utail[:], 0.0, ["utail"])
            wup = self.w_up[layer]
            ocw = CST_OFF[("ffn_cw", layer)]
            ocb = CST_OFF[("ffn_cb", layer)]
            ntile = ST // TT
            hk = [("h", i) for i in range(ntile)]
            nwst = [0]

            def load_pair(jg):
                for part in range(2):
                    s = nwst[0] % 2
                    nwst[0] += 1
                    slot = (jg % 2) * 2 + part
                    c0 = part * DFF + jg * WB
                    self.load_w(P, wup[:, c0:c0 + WB], wst[s][:], ("wst", s), wbf[slot][:], ("wbf", slot))
            load_pair(0)
            tcount = 0
            for jg in range(DFF // WB):
                if jg + 1 < DFF // WB:
                    load_pair(jg + 1)
                for jj in range(WB // 128):
                    j = jg * (WB // 128) + jj
                    zs = j % 2
                    for i in range(ntile):
                        s = tcount % 2
                        tcount += 1
                        for part, (ub, cb, name) in enumerate(((ug, cg, "g"), (uv, cv, "v"))):
                            slot = (jg % 2) * 2 + part
                            pb = self.ps[(2 if part == 0 else 4) + s]
                            kp = ("psu", part, s)
                            ku = ("u", part, s)
                            kuc = ("uc", part, s)
                            kc_ = ("c", part, s)
                            ch = part * 24 + j
                            for kc in range(8):
                                P.mm(pb[:], wbf[slot][:, kc, jj * 128:(jj + 1) * 128], h[:, kc, i * TT:(i + 1) * TT],
                                     kc == 0, kc == 7, [("wbf", slot), ("h", i)], [kp])
                            P.cp("act", ub[s][:, 2:TT + 2], pb[:], [kp], [ku])
                            if i == 0:
                                P.cp("act", ub[s][:, 0:2], self.utail[:, ch, :], ["utail", ("utl", ch)], [kuc])
                            else:
                                P.cp("act", ub[s][:, 0:2], ub[1 - s][:, TT:TT + 2], [("u", part, 1 - s)], [kuc])
                            if i == ntile - 1:
                                P.cp("pool", self.utail[:, ch, :], ub[s][:, TT:TT + 2], [ku], [("utl", ch)])
                            w0 = self.cst[:, ocw + 0 * 48 + ch:ocw + 0 * 48 + ch + 1]
                            w1 = self.cst[:, ocw + 1 * 48 + ch:ocw + 1 * 48 + ch + 1]
                            w2 = self.cst[:, ocw + 2 * 48 + ch:ocw + 2 * 48 + ch + 1]
                            bb = self.cst[:, ocb + ch:ocb + ch + 1]
                            if part == 0:
                                P.ts("dve", cb[s][:], ub[s][:, 2:TT + 2], w2, bb, ALU.mult, ALU.add, [ku, "cst"], [kc_])
                                P.stt(cb[s][:], ub[s][:, 1:TT + 1], w1, cb[s][:], ALU.mult, ALU.add, [ku, kuc, kc_], [kc_])
                                P.stt(cb[s][:], ub[s][:, 0:TT], w0, cb[s][:], ALU.mult, ALU.add, [ku, kuc, kc_], [kc_])
                            else:
                                ktm = ("ctmp", s)
                                P.ts("pool", cb[s][:], ub[s][:, 2:TT + 2], w2, bb, ALU.mult, ALU.add, [ku, "cst"], [kc_])
                                P.ts("pool", ctmp[s][:], ub[s][:, 1:TT + 1], w1, None, ALU.mult, None, [ku, kuc, "cst"], [ktm])
                                P.tt("pool", cb[s][:], cb[s][:], ctmp[s][:], ALU.add, [kc_, ktm], [kc_])
                                P.ts("pool", ctmp[s][:], ub[s][:, 0:TT], w0, None, ALU.mult, None, [ku, kuc, "cst"], [ktm])
                                P.tt("pool", cb[s][:], cb[s][:], ctmp[s][:], ALU.add, [kc_, ktm], [kc_])
                        P.act(gl[s][:], cg[s][:], AF.Gelu_apprx_tanh, [("c", 0, s)], [("gl", s)])
                        P.tt("dve", zst[zs][:, i * TT:(i + 1) * TT], gl[s][:], cv[s][:], ALU.mult,
                             [("gl", s), ("c", 1, s)], [("zst", zs, i)])
                    P.dma(self.zT[j * 128:(j + 1) * 128, st * ST:(st + 1) * ST], zst[zs][:],
                          [("zst", zs, i) for i in range(ntile)], [], ("zsts", zs))
            P.emit()

    def sub_attn(self, layer, st, src, dst):
        j = layer // 2
        with ExitStack() as sub:
            h = self.sb(sub, [128, 8, ST], BF16, "h_att%d_%d" % (layer, st))
            self.phase_norm(src, st, ("mix_pre", layer), h)
            self.phase_qkv(j, st, h)
        self.phase_attcore(st)
        self.phase_down(self.zT[0:D, :], 8, self.w_o[j], ("mix_post", layer), src, dst, st)

    def hperm(self, h, kc, g, pos, n):
        d = DIL[g]
        U = ST // d
        if d == 1:
            return h[:, kc, pos:pos + n]
        hv = h[:, kc, :].rearrange("p (u r) -> p r u", r=d)
        r0, u0 = pos // U, pos % U
        if u0 + n <= U:
            return hv[:, r0, u0:u0 + n]
        assert u0 == 0 and n % U == 0
        return hv[:, r0:r0 + n // U, :]

    def phase_qkv(self, j, st, h):
        WB = 256
        with ExitStack() as ph:
            wst = [self.sb(ph, [128, 8, 512], F32) for _ in range(2)]
            wbf = [self.sb(ph, [128, 8, WB], BF16) for _ in range(2)]
            wv = [self.sb(ph, [128, 8, 512], BF16) for _ in range(2)]
            qst = [self.sb(ph, [128, ST], BF16) for _ in range(2)]
            vst = [self.sb(ph, [128, 8, 8, 128], BF16) for _ in range(2)]
            P = Phase(self.S)
            wq = self.w_qkv[j]
            ntile = ST // TT
            nw = [0]

            def ldw(col0, ncol, dst, dkey):
                s = nw[0] % 2
                nw[0] += 1
                self.load_w(P, wq[:, col0:col0 + ncol], wst[s][:, :, 0:ncol], ("wst", s), dst, dkey)
            hcount = 0
            ecount = 0
            hall = [("h", t) for t in range(ntile)]
            for g in range(3):
                d = DIL[g]
                U = ST // d
                blocks = [(jq, hb) for jq in range(2) for hb in range(4)]
                ldw(((g * 3 + 0) * 8 + 0) * 128, WB, wbf[0][:], ("wbf", 0))
                for bi, (jq, hb) in enumerate(blocks):
                    slot = bi % 2
                    if bi + 1 < len(blocks):
                        jq2, hb2 = blocks[bi + 1]
                        ldw(((g * 3 + jq2) * 8 + hb2 * 2) * 128, WB, wbf[(bi + 1) % 2][:], ("wbf", (bi + 1) % 2))
                    elif g + 1 <= 2:
                        pass
                    for hh in range(2):
                        hd = hb * 2 + hh
                        qs = hcount % 2
                        hcount += 1
                        for i in range(ntile):
                            pb = self.ps[2 + ecount % 4]
                            kp = ("psq", ecount % 4)
                            for kc in range(8):
                                P.mm(pb[:], wbf[slot][:, kc, hh * 128:(hh + 1) * 128], h[:, kc, i * TT:(i + 1) * TT],
                                     kc == 0, kc == 7, [("wbf", slot), ("h", i)], [kp])
                            eng = "act" if ecount % 2 == 0 else "dve"
                            if d == 1:
                                P.cp(eng, qst[qs][:, i * TT:(i + 1) * TT], pb[:], [kp], [("qst", qs, i)])
                            else:
                                w_ = TT // d
                                outv = qst[qs][:].rearrange("p (r u) -> p r u", r=d)[:, :, i * w_:(i + 1) * w_]
                                inv = pb[:].rearrange("p (u r) -> p r u", r=d)
                                P.cp(eng, outv, inv, [kp], [("qst", qs, i)])
                            ecount += 1
                        dstT = self.QT if jq == 0 else self.KT
                        P.dma(dstT[g, hd, :, st * ST:(st + 1) * ST], qst[qs][:], [("qst", qs, i) for i in range(ntile)], [], ("qsts", qs))
                for half in range(2):
                    ldw(((g * 3 + 2) * 8 + half * 4) * 128, 512, wv[half][:], ("wv", half))
                nblk = ST // 128
                for b in range(nblk):
                    vs = (b // 8) % 2
                    for half in range(2):
                        pb = self.ps[2 + ecount % 4]
                        kp = ("psq", ecount % 4)
                        for kc in range(8):
                            lhsT = self.hperm(h, kc, g, b * 128, 128)
                            P.mm(pb[:], lhsT, wv[half][:, kc, :], kc == 0, kc == 7, [("wv", half)] + hall, [kp])
                        eng = "act" if ecount % 2 == 0 else "dve"
                        P.cp(eng, vst[vs][:, half * 4:(half + 1) * 4, b % 8, :],
                             pb[:].rearrange("p (a b) -> p a b", a=4), [kp], [("vst", vs, b % 8, half)])
                        ecount += 1
                    if b % 8 == 7:
                        b0 = st * nblk + b - 7
                        P.dma(self.V[g, :, :, b0:b0 + 8, :].rearrange("h p b d -> p h b d"), vst[vs][:],
                              [("vst", vs, bb, hf) for bb in range(8) for hf in range(2)], [], ("vsts", vs))
            P.emit()

    def phase_attcore(self, st):
        with ExitStack() as ph:
            qt = [self.sb(ph, [128, ST], BF16) for _ in range(2)]
            kt = [self.sb(ph, [128, ST], BF16) for _ in range(2)]
            kh = [self.sb(ph, [128, 16, 128], BF16) for _ in range(2)]
            vt = [self.sb(ph, [128, ST // 128, 128], BF16) for _ in range(2)]
            vh = [self.sb(ph, [128, 16, 128], BF16) for _ in range(2)]
            acc = self.sb(ph, [128, 2, ST], F32)
            E = self.sb(ph, [128, 3, 8, 256], F32)
            ex = [self.sb(ph, [128, 256], F32) for _ in range(4)]
            pT = [self.sb(ph, [128, 256], BF16) for _ in range(6)]
            ost = [self.sb(ph, [128, ST], BF16) for _ in range(2)]
            rs = [self.sb(ph, [128, TT], F32) for _ in range(2)]
            P = Phase(self.S)
            Ef = E[:].rearrange("p a b c -> p (a b c)")
            P.dma(Ef, self.bm_d[:, :], (), ["E"], "E")
            P.act(Ef, Ef, AF.Exp, ["E"], ["E"])
            units = [(hd, g) for hd in range(8) for g in range(3)]
            nblk = ST // 128

            def load_unit(ui):
                hd, g = units[ui]
                s = ui % 2
                d = DIL[g]
                U = ST // d
                nb = U // 128
                P.dma(qt[s][:], self.QT[g, hd, :, st * ST:(st + 1) * ST], (), [("qt", s)], ("qt", s))
                P.dma(kt[s][:], self.KT[g, hd, :, st * ST:(st + 1) * ST], (), [("kt", s)], ("kt", s))
                P.dma(vt[s][:], self.V[g, hd, :, st * nblk:(st + 1) * nblk, :], (), [("vt", s)], ("vt", s))
                if st > 0:
                    src = self.KT[g, hd, :, (st - 1) * ST:st * ST].rearrange("p (r u) -> p r u", r=d)[:, :, U - 128:U]
                    if d == 16:
                        P.dma(kh[s][:, 0:8, :], src[:, 0:8, :], (), [("kh", s, 0)], ("kh", s, 0))
                        P.dma(kh[s][:, 8:16, :], src[:, 8:16, :], (), [("kh", s, 1)], ("kh", s, 1))
                    else:
                        P.dma(kh[s][:, 0:d, :], src, (), [("kh", s, 0)], ("kh", s, 0))
                    srcv = self.V[g, hd, :, (st - 1) * nblk:st * nblk, :].rearrange("p (r n) e -> p r n e", r=d)[:, :, nb - 1, :]
                    if d == 16:
                        P.dma(vh[s][:, 0:8, :], srcv[:, 0:8, :], (), [("vh", s, 0)], ("vh", s, 0))
                        P.dma(vh[s][:, 8:16, :], srcv[:, 8:16, :], (), [("vh", s, 1)], ("vh", s, 1))
                    else:
                        P.dma(vh[s][:, 0:d, :], srcv, (), [("vh", s, 0)], ("vh", s, 0))
            load_unit(0)
            kcount = 0
            qcount = 0
            fcount = 0
            pending = []
            DLY = 2

            def flush(keep):
                while len(pending) > keep:
                    pending.pop(0)()
            for ui, (hd, g) in enumerate(units):
                if ui + 1 < len(units):
                    load_unit(ui + 1)
                s = ui % 2
                d = DIL[g]
                U = ST // d
                nb = U // 128
                hk_ = [0, 1] if d == 16 else [0]
                inres = [("qt", s), ("kt", s)] + ([("kh", s, q_) for q_ in hk_] if st > 0 else [])
                vres = [("vt", s)] + ([("vh", s, q_) for q_ in hk_] if st > 0 else [])
                for r in range(d):
                    prev = None
                    for n in range(-1 if st > 0 else 0, nb):
                        lo = n >= 0
                        hi = n + 1 < nb
                        kblk = kt[s][:, r * U + n * 128: r * U + (n + 1) * 128] if lo else kh[s][:, r, :]
                        qstart = r * U + (n if lo else n + 1) * 128
                        N = 128 * (int(lo) + int(hi))
                        c0 = 0 if lo else 128
                        ks4 = kcount % 4
                        ks = kcount % 6
                        es = kcount % 4
                        kcount += 1
                        pb = self.ps[ks4]
                        P.mm(pb[:, c0:c0 + N], kblk, qt[s][:, qstart:qstart + N], True, True, inres, [("pss", ks4)])
                        P.act(ex[es][:, c0:c0 + N], pb[:, c0:c0 + N], AF.Exp, [("pss", ks4)], [("ex", es)], scale=SCALE)
                        P.tt("pool", pT[ks][:, c0:c0 + N], ex[es][:, c0:c0 + N], E[:, g, hd, c0:c0 + N], ALU.mult,
                             [("ex", es), "E"], [("pT", ks)])
                        if lo:
                            m = n
                            parts = []
                            if prev is not None:
                                vprev = vt[s][:, r * nb + m - 1, :] if m - 1 >= 0 else vh[s][:, r, :]
                                parts.append((vprev, pT[prev][:, 128:256], ("pT", prev)))
                            parts.append((vt[s][:, r * nb + m, :], pT[ks][:, 0:128], ("pT", ks)))
                            def pvtask(parts=parts, m=m, r=r, d=d, g=g, vres=vres):
                                nonlocal qcount
                                qs_ = qcount % 4
                                qcount += 1
                                po = self.ps[4 + qs_]
                                for pi, (vb, pp, pk) in enumerate(parts):
                                    P.mm(po[:, 0:128], vb, pp, pi == 0, pi == len(parts) - 1, vres + [pk], [("pso", qs_)])
                                for pi, (vb, pp, pk) in enumerate(parts):
                                    P.mm(po[:, 128:256], self.ones[:], pp, pi == 0, pi == len(parts) - 1, ["ones", pk], [("pso", qs_)])
                                t0 = r + m * 128 * d
                                av = acc[:, :, t0:t0 + 127 * d + 1:d] if d > 1 else acc[:, :, t0:t0 + 128]
                                pv = po[:, 0:256].rearrange("p (a b) -> p a b", a=2)
                                nat = sorted(set(range(t0 // 128, (t0 + 128 * d - 1) // 128 + 1)))
                                ares = [("acc", b_) for b_ in nat]
                                if g == 0:
                                    P.cp("dve", av, pv, [("pso", qs_)], ares)
                                else:
                                    P.tt("dve", av, av, pv, ALU.add, [("pso", qs_)] + ares, ares)
                            pending.append(pvtask)
                            flush(DLY)
                        prev = ks
                flush(0)
                if g == 2:
                    os_ = hd % 2
                    for i in range(ST // TT):
                        fs = fcount % 2
                        fcount += 1
                        ares = [("acc", b_) for b_ in range(i * 4, i * 4 + 4)]
                        P.recip(rs[fs][:], acc[:, 1, i * TT:(i + 1) * TT], ares, [("rs", fs)])
                        P.tt("dve", ost[os_][:, i * TT:(i + 1) * TT], acc[:, 0, i * TT:(i + 1) * TT], rs[fs][:], ALU.mult,
                             ares + [("rs", fs)], [("ost", os_, i)])
                    P.dma(self.zT[hd * 128:(hd + 1) * 128, st * ST:(st + 1) * ST], ost[os_][:],
                          [("ost", os_, i) for i in range(ST // TT)], [], ("osts", os_))
            P.emit()

    def sub_lru(self, layer, st, src, dst):
        j = layer // 2
        with ExitStack() as sub:
            h = self.sb(sub, [128, 8, ST], BF16, "h_lru%d_%d" % (layer, st))
            self.phase_norm(src, st, ("mix_pre", layer), h)
            self.phase_lru(j, st, h)
        self.phase_down(self.zT[0:D, :], 8, self.w_out[j], ("mix_post", layer), src, dst, st)

    def phase_lru(self, j, st, h):
        with ExitStack() as ph:
            wst = [self.sb(ph, [128, 8, 256], F32) for _ in range(2)]
            wgst = [self.sb(ph, [128, 2, 256], F32) for _ in range(2)]
            wx = [self.sb(ph, [128, 8, 256], BF16) for _ in range(2)]
            wg = [self.sb(ph, [128, 8, 256], BF16) for _ in range(2)]
            wga = [self.sb(ph, [128, 2, 256], BF16) for _ in range(2)]
            wgx = [self.sb(ph, [128, 2, 256], BF16) for _ in range(2)]
            NS = 3
            xr = [[self.sb(ph, [128, TT + 3], F32) for _ in range(NS)] for _ in range(2)]
            xc = [[self.sb(ph, [128, TT], F32) for _ in range(NS)] for _ in range(2)]
            xcb = [[self.sb(ph, [128, TT], BF16) for _ in range(NS)] for _ in range(2)]
            rt = [self.sb(ph, [128, TT], F32) for _ in range(2)]
            it = [self.sb(ph, [128, TT], F32) for _ in range(2)]
            at = [self.sb(ph, [128, TT], F32) for _ in range(2)]
            mt = [self.sb(ph, [128, TT], F32) for _ in range(2)]
            bt = [self.sb(ph, [128, TT], F32) for _ in range(2)]
            hs = [[self.sb(ph, [128, TT], F32) for _ in range(2)] for _ in range(2)]
            gt = [self.sb(ph, [128, TT], F32) for _ in range(2)]
            yt = [self.sb(ph, [128, TT], BF16) for _ in range(4)]
            P = Phase(self.S)
            if st == 0:
                P.memset("pool", self.xbtail[:], 0.0, ["xbtail"])
                P.memset("pool", self.hstate[:], 0.0, ["hstate"])
            win = self.w_in[j]
            ocw = CST_OFF[("lru_cw", j)]
            ocb = CST_OFF[("lru_cb", j)]
            ogab = CST_OFF[("lru_gab", j)]
            ogxb = CST_OFF[("lru_gxb", j)]
            ntile = ST // TT
            nw = [0]

            def load_block(n):
                s = n % 2
                for (col0, dstt, nm) in ((n * 256, wx[s], "wx"), (D + n * 256, wg[s], "wg")):
                    ss = nw[0] % 2
                    nw[0] += 1
                    self.load_w(P, win[:, col0:col0 + 256], wst[ss][:], ("wst", ss), dstt[:], (nm, s))
                for (wd, dstt, nm) in ((self.w_ga, wga[s], "wga"), (self.w_gx, wgx[s], "wgx")):
                    ss = nw[0] % 2
                    nw[0] += 1
                    self.load_w(P, wd[j, n], wgst[ss][:], ("wgst", ss), dstt[:], (nm, s))
            load_block(0)
            yc = [0]

            def stageA(n, i):
                ws = n % 2
                s = (n * ntile + i) % NS
                sp_ = (n * ntile + i - 1) % NS
                hk = [("h", i)]
                for cc in range(2):
                    c = 2 * n + cc
                    pb = self.ps[2 + cc]
                    kp = ("psx", cc)
                    kx = ("xr", cc, s)
                    kxc = ("xrc", cc, s)
                    for kc in range(8):
                        P.mm(pb[:], wx[ws][:, kc, cc * 128:(cc + 1) * 128], h[:, kc, i * TT:(i + 1) * TT],
                             kc == 0, kc == 7, [("wx", ws)] + hk, [kp])
                    P.cp("act", xr[cc][s][:, 3:TT + 3], pb[:], [kp], [kx])
                    if i == 0:
                        P.cp("act", xr[cc][s][:, 0:3], self.xbtail[:, c, :], ["xbtail", ("xbt", c)], [kxc])
                    else:
                        P.cp("act", xr[cc][s][:, 0:3], xr[cc][sp_][:, TT:TT + 3], [("xr", cc, sp_)], [kxc])
                    if i == ntile - 1:
                        P.cp("pool", self.xbtail[:, c, :], xr[cc][s][:, TT:TT + 3], [kx], [("xbt", c)])
                    wt = [self.cst[:, ocw + t * 8 + c:ocw + t * 8 + c + 1] for t in range(4)]
                    bb = self.cst[:, ocb + c:ocb + c + 1]
                    kcv = ("xc", cc, s)
                    P.ts("dve", xc[cc][s][:], xr[cc][s][:, 3:TT + 3], wt[3], bb, ALU.mult, ALU.add, [kx, "cst"], [kcv])
                    for t in (2, 1, 0):
                        P.stt(xc[cc][s][:], xr[cc][s][:, t:t + TT], wt[t], xc[cc][s][:], ALU.mult, ALU.add,
                              [kx, kxc, kcv], [kcv])
                    P.cp("act", xcb[cc][s][:], xc[cc][s][:], [kcv], [("xcb", cc, s)])

            def stageG(n, i):
                ws = n % 2
                for oc in range(2):
                    pg = self.ps[0 + oc]
                    kpg = ("psg", oc)
                    for kc in range(8):
                        P.mm(pg[:], wg[ws][:, kc, oc * 128:(oc + 1) * 128], h[:, kc, i * TT:(i + 1) * TT],
                             kc == 0, kc == 7, [("wg", ws), ("h", i)], [kpg])
                    P.act(gt[oc][:], pg[:], AF.Gelu_apprx_tanh, [kpg], [("gt", oc)])

            def stageB(n, i):
                ws = n % 2
                s = (n * ntile + i) % NS
                for oc in range(2):
                    pr = self.ps[4 + oc]
                    pi_ = self.ps[6 + oc]
                    for kc in range(2):
                        P.mm(pr[:], wga[ws][:, kc, oc * 128:(oc + 1) * 128], xcb[kc][s][:], kc == 0, kc == 1,
                             [("wga", ws), ("xcb", kc, s)], [("psr", oc)])
                    for kc in range(2):
                        P.mm(pi_[:], wgx[ws][:, kc, oc * 128:(oc + 1) * 128], xcb[kc][s][:], kc == 0, kc == 1,
                             [("wgx", ws), ("xcb", kc, s)], [("psi", oc)])
                for oc in range(2):
                    c = 2 * n + oc
                    q = oc
                    P.act(rt[q][:], self.ps[4 + oc][:], AF.Sigmoid, [("psr", oc), "cst"], [("rt", q)],
                          bias=self.cst[:, ogab + c:ogab + c + 1])
                    P.act(it[q][:], self.ps[6 + oc][:], AF.Sigmoid, [("psi", oc), "cst"], [("it", q)],
                          bias=self.cst[:, ogxb + c:ogxb + c + 1])
                    P.act(at[q][:], rt[q][:], AF.Exp, [("rt", q), ("ls8", j)], [("at", q)],
                          scale=self.ls8[:, j * 8 + c:j * 8 + c + 1])
                for oc in range(2):
                    q = oc
                    P.tt("dve", mt[q][:], at[q][:], at[q][:], ALU.mult, [("at", q)], [("mt", q)])
                    P.ts("dve", mt[q][:], mt[q][:], -1.0, 1.0, ALU.mult, ALU.add, [("mt", q)], [("mt", q)])
                for oc in range(2):
                    q = oc
                    P.act(mt[q][:], mt[q][:], AF.Sqrt, [("mt", q)], [("mt", q)], bias=self.tinyb[:])
                for oc in range(2):
                    c = 2 * n + oc
                    q = oc
                    P.tt("dve", bt[q][:], mt[q][:], it[q][:], ALU.mult, [("mt", q), ("it", q)], [("bt", q)])
                    P.tt("dve", bt[q][:], bt[q][:], xc[oc][s][:], ALU.mult, [("bt", q), ("xc", oc, s)], [("bt", q)])
                    hcur = hs[oc][i % 2]
                    hprev = hs[oc][(i - 1) % 2]
                    if i == 0:
                        init = self.hstate[:, c:c + 1]
                        ires = ["hstate", ("hst", c)]
                    else:
                        init = hprev[:, TT - 1:TT]
                        ires = [("hs", oc, (i - 1) % 2)]
                    P.add("dve", (lambda o_, a_, b_, i_: (lambda e: e.tensor_tensor_scan(o_, a_, b_, i_, ALU.mult, ALU.add)))(
                        hcur[:], at[q][:], bt[q][:], init), [("at", q), ("bt", q)] + ires, [("hs", oc, i % 2)])
                    if i == ntile - 1:
                        P.cp("pool", self.hstate[:, c:c + 1], hcur[:, TT - 1:TT], [("hs", oc, i % 2)], [("hst", c)])
                    ys = yc[0] % 4
                    yc[0] += 1
                    P.tt("dve", yt[ys][:], hcur[:], gt[oc][:], ALU.mult, [("hs", oc, i % 2), ("gt", oc)], [("yt", ys)])
                    P.dma(self.zT[c * 128:(c + 1) * 128, st * ST + i * TT: st * ST + (i + 1) * TT], yt[ys][:],
                          [("yt", ys)], [], ("yts", ys))

            for n in range(4):
                if n + 1 < 4:
                    load_block(n + 1)
                stageA(n, 0)
                for i in range(ntile):
                    if i + 1 < ntile:
                        stageA(n, i + 1)
                    stageG(n, i)
                    stageB(n, i)
            P.emit()


def build_program(S_tok, plan):
    nc = bass.Bass("TRN2", target_bir_lowering=False)
    b = Builder(nc, S_tok, plan)
    b.build()
    return nc


FULL_PLAN = []
for _l in range(DEPTH):
    FULL_PLAN.append(("attn" if _l % 2 == 0 else "lru", _l))
    FULL_PLAN.append(("ffn", _l))

W_NAMES = ["attn_w_qkv", "attn_w_o", "lru_w_in", "lru_ga_w", "lru_gx_w", "lru_w_out", "ffn_w_up", "ffn_w_down"]


def kernel(**inputs):
    x = np.asarray(inputs["x"], np.float32)
    B, S, _ = x.shape
    n_act = B
    nc = build_program(S, FULL_PLAN)
    cst = build_consts(inputs)
    bm = build_biasmask(inputs["rel_bias"])
    shared = {k: np.ascontiguousarray(np.asarray(inputs[k], np.float32)) for k in W_NAMES}
    in_maps = []
    for c in range(n_act):
        m = {"xT": np.ascontiguousarray(x[c].T), "cst": cst, "bm": bm}
        m.update(shared)
        in_maps.append(m)
    res = run_bass_kernel_spmd(nc, in_maps, core_ids=list(range(n_act)))
    out = np.stack([np.ascontiguousarray(res.results[c]["yT"].T) for c in range(n_act)], axis=0)
    return out.astype(np.float32)
```

```python
import numpy as np
import os
DBGD = os.environ.get('DBGD', '')
from contextlib import ExitStack
import concourse.bass as bass
import concourse.mybir as mybir
from concourse.bass_utils import run_bass_kernel_spmd

F32, BF16 = mybir.dt.float32, mybir.dt.bfloat16
AF = mybir.ActivationFunctionType
ALU = mybir.AluOpType

D = 1024
DEPTH = 4
DIL = (1, 4, 16)
DFF = 3072
ST = 4096
TT = 512
EPS = 1e-6
SCALE = 128 ** -0.5
N_CORES = 8

def _cst_layout():
    off = {}
    n = 0
    def add(name, k):
        nonlocal n
        off[name] = n
        n += k
    for kind in ("mix_pre", "mix_post", "ffn_pre", "ffn_post"):
        for l in range(DEPTH):
            add((kind, l), 8)
    for j in range(2):
        add(("lru_cw", j), 4 * 8)
        add(("lru_cb", j), 8)
        add(("lru_gab", j), 8)
        add(("lru_gxb", j), 8)
        add(("lru_lam", j), 8)
    for l in range(DEPTH):
        add(("ffn_cw", l), 3 * 48)
        add(("ffn_cb", l), 48)
    return off, n

CST_OFF, NCST = _cst_layout()


def _pm(v):
    return np.ascontiguousarray(np.asarray(v, np.float32).reshape(-1, 128).T)


def build_consts(inp):
    c = np.zeros((128, NCST), np.float32)
    names = {"mix_pre": "norm_mix_pre", "mix_post": "norm_mix_post",
             "ffn_pre": "norm_ffn_pre", "ffn_post": "norm_ffn_post"}
    for kind, nm in names.items():
        for l in range(DEPTH):
            o = CST_OFF[(kind, l)]
            c[:, o:o + 8] = _pm(inp[nm][l])
    for j in range(2):
        o = CST_OFF[("lru_cw", j)]
        for t in range(4):
            c[:, o + t * 8:o + t * 8 + 8] = _pm(inp["lru_conv_w"][j, t])
        c[:, CST_OFF[("lru_cb", j)]:CST_OFF[("lru_cb", j)] + 8] = _pm(inp["lru_conv_b"][j])
        c[:, CST_OFF[("lru_gab", j)]:CST_OFF[("lru_gab", j)] + 8] = _pm(inp["lru_ga_b"][j].reshape(-1))
        c[:, CST_OFF[("lru_gxb", j)]:CST_OFF[("lru_gxb", j)] + 8] = _pm(inp["lru_gx_b"][j].reshape(-1))
        c[:, CST_OFF[("lru_lam", j)]:CST_OFF[("lru_lam", j)] + 8] = _pm(inp["lru_lambda"][j])
    for l in range(DEPTH):
        o = CST_OFF[("ffn_cw", l)]
        for t in range(3):
            c[:, o + t * 48:o + t * 48 + 48] = _pm(inp["ffn_conv_w"][l, t])
        o = CST_OFF[("ffn_cb", l)]
        c[:, o:o + 48] = _pm(inp["ffn_conv_b"][l])
    return c


def _t5_bucket(dist):
    max_exact = 16
    d = np.maximum(dist, 1).astype(np.float64)
    large = max_exact + (np.log(d / max_exact) / np.log(2048 / max_exact) * (32 - max_exact)).astype(np.int32)
    large = np.minimum(large, 31)
    return np.where(dist < max_exact, dist, large).astype(np.int32)


def build_biasmask(rel_bias):
    rel_bias = np.asarray(rel_bias, np.float32)
    bm = np.full((128, 3, 8, 256), -1e30, np.float32)
    kk = np.arange(128)[:, None]
    ii = np.arange(128)[None, :]
    for g in range(3):
        d = DIL[g]
        m_same = ii - kk
        m_next = ii + 128 - kk
        for hd in range(8):
            tbl = rel_bias[:, g * 8 + hd]
            b_same = tbl[_t5_bucket(np.clip(m_same, 0, 128) * d)]
            b_next = tbl[_t5_bucket(np.clip(m_next, 0, 128) * d)]
            bm[:, g, hd, 0:128] = np.where(m_same >= 0, b_same, np.float32(-1e30))
            bm[:, g, hd, 128:256] = np.where(m_next <= 128, b_next, np.float32(-1e30))
    return np.ascontiguousarray(bm.reshape(128, 3 * 8 * 256))


ENGS = ("pe", "act", "dve", "pool", "sp")


class Sched:
    def __init__(self, nc, stack, n_dma=64):
        self.nc = nc
        self.esem = {e: stack.enter_context(nc.semaphore("se_" + e)) for e in ENGS}
        self.ecnt = {e: 0 for e in ENGS}
        self.dsem = [stack.enter_context(nc.semaphore("sd%d" % i)) for i in range(n_dma)]
        self.dcnt = [0] * n_dma
        self.nphase = 0


class Phase:
    def __init__(self, S):
        self.S = S
        self.ops = []
        self.lw = {}
        self.rd = {}
        self.keyidx = {}

    def add(self, eng, fn, reads=(), writes=(), dma_key=None):
        idx = len(self.ops)
        deps = set()
        for r in reads:
            w = self.lw.get(r)
            if w is not None:
                deps.add(w)
        for r in writes:
            w = self.lw.get(r)
            if w is not None:
                deps.add(w)
            rl = self.rd.get(r)
            if rl:
                deps.update(rl)
        for r in reads:
            self.rd.setdefault(r, []).append(idx)
        for r in writes:
            self.lw[r] = idx
            self.rd[r] = []
        deps.discard(idx)
        dk = None
        if dma_key is not None:
            if dma_key not in self.keyidx:
                self.keyidx[dma_key] = len(self.keyidx)
                assert len(self.keyidx) <= len(self.S.dsem), "too many dma keys"
            dk = self.keyidx[dma_key]
        self.ops.append([eng, fn, deps, dk, False, 0])
        return idx

    def mm(self, out, lhsT, rhs, start, stop, reads, writes):
        self.add("pe", lambda e: e.matmul(out, lhsT, rhs, start=start, stop=stop), reads, writes)

    def act(self, out, in_, func, reads, writes, scale=None, bias=None):
        kw = {}
        if scale is not None:
            kw["scale"] = scale
        if bias is not None:
            kw["bias"] = bias
        self.add("act", lambda e: e.activation(out=out, in_=in_, func=func, **kw), reads, writes)

    def ts(self, eng, out, in0, s1, s2, op0, op1, reads, writes):
        if op1 is None:
            self.add(eng, lambda e: e.tensor_scalar(out, in0, s1, None, op0), reads, writes)
        else:
            self.add(eng, lambda e: e.tensor_scalar(out, in0, s1, s2, op0, op1), reads, writes)

    def stt(self, out, in0, scalar, in1, op0, op1, reads, writes):
        self.add("dve", lambda e: e.scalar_tensor_tensor(out, in0, scalar, in1, op0, op1), reads, writes)

    def tt(self, eng, out, in0, in1, op, reads, writes):
        self.add(eng, lambda e: e.tensor_tensor(out, in0, in1, op), reads, writes)

    def cp(self, eng, out, in_, reads, writes):
        if eng == "act":
            self.add(eng, lambda e: e.copy(out, in_), reads, writes)
        else:
            self.add(eng, lambda e: e.tensor_copy(out, in_), reads, writes)

    def recip(self, out, in_, reads, writes):
        self.add("dve", lambda e: e.reciprocal(out, in_), reads, writes)

    def memset(self, eng, ap, val, writes):
        self.add(eng, lambda e: e.memset(ap, val), (), writes)

    def dma(self, out, in_, reads, writes, key, eng="sp"):
        self.add(eng, lambda e: e.dma_start(out=out, in_=in_), reads, writes, dma_key=key)

    def emit(self):
        S = self.S
        nc = S.nc
        ops = self.ops
        n = len(ops)
        dcount = {}
        dc = list(S.dcnt)
        for i, op in enumerate(ops):
            if op[3] is not None:
                dc[op[3]] += 1
                dcount[i] = dc[op[3]]
        need = [None] * n
        for i, op in enumerate(ops):
            eng = op[0]
            per_eng = {}
            per_dma = {}
            for y in op[2]:
                oy = ops[y]
                if oy[3] is not None:
                    k = oy[3]
                    if dcount[y] > per_dma.get(k, 0):
                        per_dma[k] = dcount[y]
                else:
                    if oy[0] == "pe" and eng == "pe":
                        continue
                    if y > per_eng.get(oy[0], -1):
                        per_eng[oy[0]] = y
            for y in per_eng.values():
                ops[y][4] = True
            need[i] = (per_eng, per_dma)
        ec = dict(S.ecnt)
        for op in ops:
            if op[3] is None and op[4]:
                ec[op[0]] += 1
                op[5] = ec[op[0]]
        per = {e: [] for e in ENGS}
        for i, op in enumerate(ops):
            per[op[0]].append(i)
        final_dma = [(k, dc[k]) for k in range(len(dc)) if dc[k] > S.dcnt[k]]

        def emit_eng(e, name):
            waited = {}
            for i in per[name]:
                op = ops[i]
                pe_, pd_ = need[i]
                for en, y in pe_.items():
                    v = ops[y][5]
                    key = ("e", en)
                    if waited.get(key, 0) < v:
                        e.wait_ge(S.esem[en], v)
                        waited[key] = v
                for k, c in pd_.items():
                    key = ("d", k)
                    if waited.get(key, 0) < c * 16:
                        e.wait_ge(S.dsem[k], c * 16)
                        waited[key] = c * 16
                ins = op[1](e)
                if op[3] is not None:
                    ins.then_inc(S.dsem[op[3]], 16)
                elif op[4]:
                    ins.then_inc(S.esem[name], 1)
            if name == "sp":
                for k, c in final_dma:
                    if waited.get(("d", k), 0) < c * 16:
                        e.wait_ge(S.dsem[k], c * 16)

        with nc.Block() as blk:
            @blk.tensor
            def _(e):
                emit_eng(e, "pe")

            @blk.scalar
            def _(e):
                emit_eng(e, "act")

            @blk.vector
            def _(e):
                emit_eng(e, "dve")

            @blk.gpsimd
            def _(e):
                emit_eng(e, "pool")

            @blk.sync
            def _(e):
                emit_eng(e, "sp")
        S.ecnt = ec
        S.dcnt = dc
        S.nphase += 1


class Builder:
    def __init__(self, nc, S_tok, plan):
        self.nc = nc
        self.S_tok = S_tok
        self.NT = S_tok // ST
        self.plan = plan
        self._n = 0

    def sb(self, stack, shape, dt, name=None):
        self._n += 1
        nm = name or ("t%d" % self._n)
        t = stack.enter_context(self.nc.sbuf_tensor(nm, shape, dt))
        return t

    def dram_in(self, name, shape, dt=F32):
        return self.nc.dram_tensor(name, list(shape), dt, kind="ExternalInput").ap()

    def build(self):
        nc = self.nc
        S_tok = self.S_tok
        self.x_in = self.dram_in("xT", [D, S_tok])
        self.y_out = nc.dram_tensor("yT", [D, S_tok], F32, kind="ExternalOutput").ap()
        self.cst_d = self.dram_in("cst", [128, NCST])
        self.bm_d = self.dram_in("bm", [128, 3 * 8 * 256])
        self.w_qkv = self.dram_in("attn_w_qkv", [2, D, 9216])
        self.w_o = self.dram_in("attn_w_o", [2, D, D])
        self.w_in = self.dram_in("lru_w_in", [2, D, 2 * D])
        self.w_ga = self.dram_in("lru_ga_w", [2, 4, 256, 256])
        self.w_gx = self.dram_in("lru_gx_w", [2, 4, 256, 256])
        self.w_out = self.dram_in("lru_w_out", [2, D, D])
        self.w_up = self.dram_in("ffn_w_up", [DEPTH, D, 2 * DFF])
        self.w_down = self.dram_in("ffn_w_down", [DEPTH, DFF, D])
        self.xs = nc.dram_tensor("xs", [D, S_tok], F32, kind="Internal").ap()
        self.zT = nc.dram_tensor("zT", [DFF, S_tok], BF16, kind="Internal").ap()
        self.QT = nc.dram_tensor("QT", [3, 8, 128, S_tok], BF16, kind="Internal").ap()
        self.KT = nc.dram_tensor("KT", [3, 8, 128, S_tok], BF16, kind="Internal").ap()
        self.V = nc.dram_tensor("Vs", [3, 8, 128, S_tok // 128, 128], BF16, kind="Internal").ap()

        with ExitStack() as top:
            self.S = Sched(nc, top)
            self.ps = [top.enter_context(nc.psum_tensor("ps%d" % i, [128, 512], F32)) for i in range(8)]
            self.cst = self.sb(top, [128, NCST], F32, "cst_sb")
            self.ones = self.sb(top, [128, 128], BF16, "ones")
            self.utail = self.sb(top, [128, 48, 2], F32, "utail")
            self.xbtail = self.sb(top, [128, 8, 3], F32, "xbtail")
            self.hstate = self.sb(top, [128, 8], F32, "hstate")
            self.ls8 = self.sb(top, [128, 16], F32, "ls8")
            self.epsb = self.sb(top, [128, 1], F32, "epsb")
            self.tinyb = self.sb(top, [128, 1], F32, "tinyb")
            self.phase_init()
            cur = self.x_in
            nsub = len(self.plan)
            for si, (kind, layer) in enumerate(self.plan):
                dst = self.y_out if si == nsub - 1 else self.xs
                for st in range(self.NT):
                    if kind == "attn":
                        self.sub_attn(layer, st, cur, dst)
                    elif kind == "lru":
                        self.sub_lru(layer, st, cur, dst)
                    elif kind == "ffn":
                        self.sub_ffn(layer, st, cur, dst)
                cur = dst
        return nc

    def ccol(self, key, i=0, n=1):
        o = CST_OFF[key] + i
        return self.cst[:, o:o + n]

    def phase_init(self):
        P = Phase(self.S)
        P.dma(self.cst[:], self.cst_d[:, :], (), ["cst"], "cst")
        P.memset("dve", self.ones[:], 1.0, ["ones"])
        P.memset("dve", self.epsb[:], EPS, ["epsb"])
        P.memset("dve", self.tinyb[:], 1e-30, ["tinyb"])
        for j in range(2):
            o = CST_OFF[("lru_lam", j)]
            dst = self.ls8[:, j * 8:(j + 1) * 8]
            P.act(dst, self.cst[:, o:o + 8], AF.Exp, ["cst"], [("ls8", j)], scale=-1.0)
            P.ts("dve", dst, dst, 1.0, None, ALU.add, None, [("ls8", j)], [("ls8", j)])
            P.act(dst, dst, AF.Ln, [("ls8", j)], [("ls8", j)])
            P.ts("dve", dst, dst, -8.0, None, ALU.mult, None, [("ls8", j)], [("ls8", j)])
        P.emit()

    def load_w(self, P, w_ap, stage, stage_key, dst_ap, dst_key):
        P.dma(stage, w_ap.rearrange("(kc p) c -> p kc c", p=128), (), [stage_key], stage_key)
        P.cp("pool", dst_ap, stage, [stage_key], [dst_key])

    def phase_norm(self, src, st, gkey, h):
        with ExitStack() as ph:
            xt = [self.sb(ph, [128, 8, TT], F32) for _ in range(2)]
            sq = [self.sb(ph, [128, 8, TT], BF16) for _ in range(2)]
            sd = [self.sb(ph, [128, TT], F32) for _ in range(2)]
            P = Phase(self.S)
            psn = [self.ps[0], self.ps[1]]
            for i in range(ST // TT):
                s = i % 2
                c0 = st * ST + i * TT
                kx, kq, kd, kp = ("nx", s), ("nq", s), ("nd", s), ("psn", s)
                P.dma(xt[s][:], src[:, c0:c0 + TT].rearrange("(kc p) t -> p kc t", p=128), (), [kx], kx)
                P.act(sq[s][:], xt[s][:], AF.Square, [kx], [kq])
                for kc in range(8):
                    P.mm(psn[s][:], self.ones[:], sq[s][:, kc, :], kc == 0, kc == 7, [kq, "ones"], [kp])
                P.act(sd[s][:], psn[s][:], AF.Sqrt, [kp], [kd], scale=1.0 / D, bias=self.epsb[:])
                P.recip(sd[s][:], sd[s][:], [kd], [kd])
                for kc in range(8):
                    P.stt(h[:, kc, i * TT:(i + 1) * TT], xt[s][:, kc, :], self.ccol(gkey, kc), sd[s][:],
                          ALU.mult, ALU.mult, [kx, kd, "cst"], [("h", i)])
            P.emit()

    def phase_down(self, z_d, KC, w_d, gkey, src, dst, st):
        with ExitStack() as ph:
            wres = self.sb(ph, [128, KC, D], BF16)
            wst = [self.sb(ph, [128, 4, D], F32) for _ in range(2)]
            zt = [self.sb(ph, [128, KC, TT], BF16) for _ in range(2)]
            xt = self.sb(ph, [128, 8, TT], F32)
            ot = self.sb(ph, [128, 8, TT], F32)
            sq = self.sb(ph, [128, 8, TT], BF16)
            sd = self.sb(ph, [128, TT], F32)
            P = Phase(self.S)
            for q in range(KC // 4):
                s = q % 2
                if 'w' in DBGD:
                    continue
                if 'W' in DBGD:
                    P.dma(wst[s][:], w_d[q * 512:(q + 1) * 512, :].rearrange("(kc p) c -> p kc c", p=128), (), [("wst", s)], ("wst", s))
                    continue
                self.load_w(P, w_d[q * 512:(q + 1) * 512, :], wst[s][:], ("wst", s),
                            wres[:, q * 4:(q + 1) * 4, :], ("wres", q))
            wkeys = [("wres", q) for q in range(KC // 4)]
            ntile = ST // TT

            def load_z(i):
                c0 = st * ST + i * TT
                for q8 in range(KC // 8):
                    P.dma(zt[i % 2][:, q8 * 8:(q8 + 1) * 8, :],
                          z_d[q8 * 1024:(q8 + 1) * 1024, c0:c0 + TT].rearrange("(kc p) t -> p kc t", p=128), (),
                          [("zt", i % 2, q8)], ("zt", i % 2, q8))
            if 'z' not in DBGD:
                load_z(0)
            for i in range(ntile):
                if i + 1 < ntile and 'z' not in DBGD:
                    load_z(i + 1)
                s = i % 2
                c0 = st * ST + i * TT
                P.dma(xt[:], src[:, c0:c0 + TT].rearrange("(kc p) t -> p kc t", p=128), (), ["dx"], "dx")
                lvl = int(os.environ.get('DBGL', '9'))
                for oc in range(8 if lvl >= 2 else 0):
                    pb = self.ps[2 + oc % 4]
                    kp = ("pso", oc % 4)
                    for kc in range(KC if 'm' not in DBGD else 1):
                        P.mm(pb[:], wres[:, kc, oc * 128:(oc + 1) * 128], zt[s][:, kc, :], kc == 0, kc == (KC - 1 if 'm' not in DBGD else 0),
                             [("zt", s, kc // 8)] + wkeys, [kp])
                    P.act(sq[:, oc, :], pb[:], AF.Square, [kp], [("dsq", oc)])
                    P.cp("dve", ot[:, oc, :], pb[:], [kp, ("dsq", oc)], [("dot", oc)])
                for kc in range(8 if lvl >= 3 else 0):
                    P.mm(self.ps[0][:], self.ones[:], sq[:, kc, :], kc == 0, kc == 7, [("dsq", kc), "ones"], ["psn"])
                if lvl >= 3:
                    P.act(sd[:], self.ps[0][:], AF.Sqrt, ["psn"], ["dsd"], scale=1.0 / D, bias=self.epsb[:])
                    P.recip(sd[:], sd[:], ["dsd"], ["dsd"])
                for oc in range(8 if lvl >= 4 else 0):
                    P.stt(ot[:, oc, :], ot[:, oc, :], self.ccol(gkey, oc), sd[:], ALU.mult, ALU.mult,
                          [("dot", oc), "dsd", "cst"], [("dot", oc)])
                    if lvl >= 5:
                        P.tt("pool", xt[:, oc, :], xt[:, oc, :], ot[:, oc, :], ALU.add, [("dot", oc), "dx"], [("dxo", oc)])
                P.dma(dst[:, c0:c0 + TT].rearrange("(kc p) t -> p kc t", p=128), xt[:],
                      [("dxo", oc) for oc in range(8)] + ["dx"], [], "dxs")
            P.emit()

    def sub_ffn(self, layer, st, src, dst):
        with ExitStack() as sub:
            h = self.sb(sub, [128, 8, ST], BF16, "h_ffn%d_%d" % (layer, st))
            self.phase_norm(src, st, ("ffn_pre", layer), h)
            self.phase_ffn_up(layer, st, h)
        self.phase_down(self.zT, 24, self.w_down[layer], ("ffn_post", layer), src, dst, st)

    def phase_ffn_up(self, layer, st, h):
        WB = 256
        with ExitStack() as ph:
            wst = [self.sb(ph, [128, 8, WB], F32) for _ in range(2)]
            wbf = [self.sb(ph, [128, 8, WB], BF16) for _ in range(4)]
            ug = [self.sb(ph, [128, TT + 2], F32) for _ in range(2)]
            uv = [self.sb(ph, [128, TT + 2], F32) for _ in range(2)]
            cg = [self.sb(ph, [128, TT], F32) for _ in range(2)]
            cv = [self.sb(ph, [128, TT], F32) for _ in range(2)]
            gl = [self.sb(ph, [128, TT], F32) for _ in range(2)]
            ctmp = [self.sb(ph, [128, TT], F32) for _ in range(2)]
            zst = [self.sb(ph, [128, ST], BF16) for _ in range(2)]
            P = Phase(self.S)
            if st == 0:
                P.memset("pool", self.utail[:], 0.0, ["utail"])
            wup = self.w_up[layer]
            ocw = CST_OFF[("ffn_cw", layer)]
            ocb = CST_OFF[("ffn_cb", layer)]
            ntile = ST // TT
            hk = [("h", i) for i in range(ntile)]
            nwst = [0]

            def load_pair(jg):
                for part in range(2):
                    s = nwst[0] % 2
                    nwst[0] += 1
                    slot = (jg % 2) * 2 + part
                    c0 = part * DFF + jg * WB
                    self.load_w(P, wup[:, c0:c0 + WB], wst[s][:], ("wst", s), wbf[slot][:], ("wbf", slot))
            load_pair(0)
            tcount = 0
            for jg in range(DFF // WB):
                if jg + 1 < DFF // WB:
                    load_pair(jg + 1)
                for jj in range(WB // 128):
                    j = jg * (WB // 128) + jj
                    zs = j % 2
                    for i in range(ntile):
                        s = tcount % 2
                        tcount += 1
                        info = []
                        for part, (ub, cb, name) in enumerate(((ug, cg, "g"), (uv, cv, "v"))):
                            slot = (jg % 2) * 2 + part
                            pb = self.ps[(2 if part == 0 else 4) + s]
                            kp = ("psu", part, s)
                            ku = ("u", part, s)
                            kuc = ("uc", part, s)
                            kc_ = ("c", part, s)
                            ch = part * 24 + j
                            for kc in range(8):
                                P.mm(pb[:], wbf[slot][:, kc, jj * 128:(jj + 1) * 128], h[:, kc, i * TT:(i + 1) * TT],
                                     kc == 0, kc == 7, [("wbf", slot), ("h", i)], [kp])
                            w0 = self.cst[:, ocw + 0 * 48 + ch:ocw + 0 * 48 + ch + 1]
                            w1 = self.cst[:, ocw + 1 * 48 + ch:ocw + 1 * 48 + ch + 1]
                            w2 = self.cst[:, ocw + 2 * 48 + ch:ocw + 2 * 48 + ch + 1]
                            bb = self.cst[:, ocb + ch:ocb + ch + 1]
                            P.cp("act", ub[s][:, 2:TT + 2], pb[:], [kp], [ku])
                            P.act(cb[s][:], pb[:], AF.Identity, [kp, "cst"], [kc_], scale=w2, bias=bb)
                            if i == 0:
                                P.cp("act", ub[s][:, 0:2], self.utail[:, ch, :], ["utail", ("utl", ch)], [kuc])
                            else:
                                P.cp("act", ub[s][:, 0:2], ub[1 - s][:, TT:TT + 2], [("u", part, 1 - s)], [kuc])
                            if i == ntile - 1:
                                P.cp("pool", self.utail[:, ch, :], ub[s][:, TT:TT + 2], [ku], [("utl", ch)])
                            info.append((ub, cb, ku, kuc, kc_, w0, w1))
                        for tap in (1, 0):
                            for (ub, cb, ku, kuc, kc_, w0, w1) in info:
                                wv_ = w1 if tap == 1 else w0
                                P.stt(cb[s][:], ub[s][:, tap:tap + TT], wv_, cb[s][:], ALU.mult, ALU.add, [ku, kuc, kc_, "cst"], [kc_])
                        P.act(gl[s][:], cg[s][:], AF.Gelu_apprx_tanh, [("c", 0, s)], [("gl", s)])
                        P.tt("dve", zst[zs][:, i * TT:(i + 1) * TT], gl[s][:], cv[s][:], ALU.mult,
                             [("gl", s), ("c", 1, s)], [("zst", zs, i)])
                    P.dma(self.zT[j * 128:(j + 1) * 128, st * ST:(st + 1) * ST], zst[zs][:],
                          [("zst", zs, i) for i in range(ntile)], [], ("zsts", zs))
            P.emit()

    def sub_attn(self, layer, st, src, dst):
        j = layer // 2
        with ExitStack() as sub:
            h = self.sb(sub, [128, 8, ST], BF16, "h_att%d_%d" % (layer, st))
            self.phase_norm(src, st, ("mix_pre", layer), h)
            self.phase_qkv(j, st, h)
        self.phase_attcore(st)
        self.phase_down(self.zT[0:D, :], 8, self.w_o[j], ("mix_post", layer), src, dst, st)

    def hperm(self, h, kc, g, pos, n):
        d = DIL[g]
        U = ST // d
        if d == 1:
            return h[:, kc, pos:pos + n]
        hv = h[:, kc, :].rearrange("p (u r) -> p r u", r=d)
        r0, u0 = pos // U, pos % U
        if u0 + n <= U:
            return hv[:, r0, u0:u0 + n]
        assert u0 == 0 and n % U == 0
        return hv[:, r0:r0 + n // U, :]

    def phase_qkv(self, j, st, h):
        WB = 256
        with ExitStack() as ph:
            wst = [self.sb(ph, [128, 8, 512], F32) for _ in range(2)]
            wbf = [self.sb(ph, [128, 8, WB], BF16) for _ in range(2)]
            wv = [self.sb(ph, [128, 8, 512], BF16) for _ in range(2)]
            qst = [self.sb(ph, [128, ST], BF16) for _ in range(2)]
            vst = [self.sb(ph, [128, 8, 8, 128], BF16) for _ in range(2)]
            P = Phase(self.S)
            wq = self.w_qkv[j]
            ntile = ST // TT
            nw = [0]

            def ldw(col0, ncol, dst, dkey):
                s = nw[0] % 2
                nw[0] += 1
                self.load_w(P, wq[:, col0:col0 + ncol], wst[s][:, :, 0:ncol], ("wst", s), dst, dkey)
            hcount = 0
            ecount = 0
            hall = [("h", t) for t in range(ntile)]
            for g in range(3):
                d = DIL[g]
                U = ST // d
                blocks = [(jq, hb) for jq in range(2) for hb in range(4)]
                ldw(((g * 3 + 0) * 8 + 0) * 128, WB, wbf[0][:], ("wbf", 0))
                for bi, (jq, hb) in enumerate(blocks):
                    slot = bi % 2
                    if bi + 1 < len(blocks):
                        jq2, hb2 = blocks[bi + 1]
                        ldw(((g * 3 + jq2) * 8 + hb2 * 2) * 128, WB, wbf[(bi + 1) % 2][:], ("wbf", (bi + 1) % 2))
                    elif g + 1 <= 2:
                        pass
                    for hh in range(2):
                        hd = hb * 2 + hh
                        qs = hcount % 2
                        hcount += 1
                        for i in range(ntile):
                            pb = self.ps[2 + ecount % 4]
                            kp = ("psq", ecount % 4)
                            for kc in range(8):
                                P.mm(pb[:], wbf[slot][:, kc, hh * 128:(hh + 1) * 128], h[:, kc, i * TT:(i + 1) * TT],
                                     kc == 0, kc == 7, [("wbf", slot), ("h", i)], [kp])
                            eng = "act" if ecount % 2 == 0 else "dve"
                            if d == 1:
                                P.cp(eng, qst[qs][:, i * TT:(i + 1) * TT], pb[:], [kp], [("qst", qs, i)])
                            else:
                                w_ = TT // d
                                outv = qst[qs][:].rearrange("p (r u) -> p r u", r=d)[:, :, i * w_:(i + 1) * w_]
                                inv = pb[:].rearrange("p (u r) -> p r u", r=d)
                                P.cp(eng, outv, inv, [kp], [("qst", qs, i)])
                            ecount += 1
                        dstT = self.QT if jq == 0 else self.KT
                        P.dma(dstT[g, hd, :, st * ST:(st + 1) * ST], qst[qs][:], [("qst", qs, i) for i in range(ntile)], [], ("qsts", qs))
                for half in range(2):
                    ldw(((g * 3 + 2) * 8 + half * 4) * 128, 512, wv[half][:], ("wv", half))
                nblk = ST // 128
                for b in range(nblk):
                    vs = (b // 8) % 2
                    for half in range(2):
                        pb = self.ps[2 + ecount % 4]
                        kp = ("psq", ecount % 4)
                        for kc in range(8):
                            lhsT = self.hperm(h, kc, g, b * 128, 128)
                            P.mm(pb[:], lhsT, wv[half][:, kc, :], kc == 0, kc == 7, [("wv", half)] + hall, [kp])
                        eng = "act" if ecount % 2 == 0 else "dve"
                        P.cp(eng, vst[vs][:, half * 4:(half + 1) * 4, b % 8, :],
                             pb[:].rearrange("p (a b) -> p a b", a=4), [kp], [("vst", vs, b % 8, half)])
                        ecount += 1
                    if b % 8 == 7:
                        b0 = st * nblk + b - 7
                        P.dma(self.V[g, :, :, b0:b0 + 8, :].rearrange("h p b d -> p h b d"), vst[vs][:],
                              [("vst", vs, bb, hf) for bb in range(8) for hf in range(2)], [], ("vsts", vs))
            P.emit()

    def phase_attcore(self, st):
        with ExitStack() as ph:
            qt = [self.sb(ph, [128, ST], BF16) for _ in range(2)]
            kt = [self.sb(ph, [128, ST], BF16) for _ in range(2)]
            kh = [self.sb(ph, [128, 16, 128], BF16) for _ in range(2)]
            vt = [self.sb(ph, [128, ST // 128, 128], BF16) for _ in range(2)]
            vh = [self.sb(ph, [128, 16, 128], BF16) for _ in range(2)]
            acc = self.sb(ph, [128, 2, ST], F32)
            E = self.sb(ph, [128, 3, 8, 256], F32)
            ex = [self.sb(ph, [128, 256], F32) for _ in range(4)]
            pT = [self.sb(ph, [128, 256], BF16) for _ in range(6)]
            ost = [self.sb(ph, [128, ST], BF16) for _ in range(2)]
            rs = [self.sb(ph, [128, TT], F32) for _ in range(2)]
            P = Phase(self.S)
            Ef = E[:].rearrange("p a b c -> p (a b c)")
            P.dma(Ef, self.bm_d[:, :], (), ["E"], "E")
            P.act(Ef, Ef, AF.Exp, ["E"], ["E"])
            units = [(hd, g) for hd in range(8) for g in range(3)]
            nblk = ST // 128

            def load_unit(ui):
                hd, g = units[ui]
                s = ui % 2
                d = DIL[g]
                U = ST // d
                nb = U // 128
                P.dma(qt[s][:], self.QT[g, hd, :, st * ST:(st + 1) * ST], (), [("qt", s)], ("qt", s))
                P.dma(kt[s][:], self.KT[g, hd, :, st * ST:(st + 1) * ST], (), [("kt", s)], ("kt", s))
                P.dma(vt[s][:], self.V[g, hd, :, st * nblk:(st + 1) * nblk, :], (), [("vt", s)], ("vt", s))
                if st > 0:
                    src = self.KT[g, hd, :, (st - 1) * ST:st * ST].rearrange("p (r u) -> p r u", r=d)[:, :, U - 128:U]
                    if d == 16:
                        P.dma(kh[s][:, 0:8, :], src[:, 0:8, :], (), [("kh", s, 0)], ("kh", s, 0))
                        P.dma(kh[s][:, 8:16, :], src[:, 8:16, :], (), [("kh", s, 1)], ("kh", s, 1))
                    else:
                        P.dma(kh[s][:, 0:d, :], src, (), [("kh", s, 0)], ("kh", s, 0))
                    srcv = self.V[g, hd, :, (st - 1) * nblk:st * nblk, :].rearrange("p (r n) e -> p r n e", r=d)[:, :, nb - 1, :]
                    if d == 16:
                        P.dma(vh[s][:, 0:8, :], srcv[:, 0:8, :], (), [("vh", s, 0)], ("vh", s, 0))
                        P.dma(vh[s][:, 8:16, :], srcv[:, 8:16, :], (), [("vh", s, 1)], ("vh", s, 1))
                    else:
                        P.dma(vh[s][:, 0:d, :], srcv, (), [("vh", s, 0)], ("vh", s, 0))
            load_unit(0)
            kcount = 0
            qcount = 0
            fcount = 0
            pending = []
            DLY = 2

            def flush(keep):
                while len(pending) > keep:
                    pending.pop(0)()
            for ui, (hd, g) in enumerate(units):
                if ui + 1 < len(units):
                    load_unit(ui + 1)
                s = ui % 2
                d = DIL[g]
                U = ST // d
                nb = U // 128
                hk_ = [0, 1] if d == 16 else [0]
                inres = [("qt", s), ("kt", s)] + ([("kh", s, q_) for q_ in hk_] if st > 0 else [])
                vres = [("vt", s)] + ([("vh", s, q_) for q_ in hk_] if st > 0 else [])
                for r in range(d):
                    prev = None
                    for n in range(-1 if st > 0 else 0, nb):
                        lo = n >= 0
                        hi = n + 1 < nb
                        kblk = kt[s][:, r * U + n * 128: r * U + (n + 1) * 128] if lo else kh[s][:, r, :]
                        qstart = r * U + (n if lo else n + 1) * 128
                        N = 128 * (int(lo) + int(hi))
                        c0 = 0 if lo else 128
                        ks4 = kcount % 4
                        ks = kcount % 6
                        es = kcount % 4
                        kcount += 1
                        pb = self.ps[ks4]
                        P.mm(pb[:, c0:c0 + N], kblk, qt[s][:, qstart:qstart + N], True, True, inres, [("pss", ks4)])
                        P.act(ex[es][:, c0:c0 + N], pb[:, c0:c0 + N], AF.Exp, [("pss", ks4)], [("ex", es)], scale=SCALE)
                        P.tt("pool", pT[ks][:, c0:c0 + N], ex[es][:, c0:c0 + N], E[:, g, hd, c0:c0 + N], ALU.mult,
                             [("ex", es), "E"], [("pT", ks)])
                        if lo:
                            m = n
                            parts = []
                            if prev is not None:
                                vprev = vt[s][:, r * nb + m - 1, :] if m - 1 >= 0 else vh[s][:, r, :]
                                parts.append((vprev, pT[prev][:, 128:256], ("pT", prev)))
                            parts.append((vt[s][:, r * nb + m, :], pT[ks][:, 0:128], ("pT", ks)))
                            def pvtask(parts=parts, m=m, r=r, d=d, g=g, vres=vres):
                                nonlocal qcount
                                qs_ = qcount % 4
                                qcount += 1
                                po = self.ps[4 + qs_]
                                for pi, (vb, pp, pk) in enumerate(parts):
                                    P.mm(po[:, 0:128], vb, pp, pi == 0, pi == len(parts) - 1, vres + [pk], [("pso", qs_)])
                                for pi, (vb, pp, pk) in enumerate(parts):
                                    P.mm(po[:, 128:256], self.ones[:], pp, pi == 0, pi == len(parts) - 1, ["ones", pk], [("pso", qs_)])
                                t0 = r + m * 128 * d
                                av = acc[:, :, t0:t0 + 127 * d + 1:d] if d > 1 else acc[:, :, t0:t0 + 128]
                                pv = po[:, 0:256].rearrange("p (a b) -> p a b", a=2)
                                nat = sorted(set(range(t0 // 128, (t0 + 128 * d - 1) // 128 + 1)))
                                ares = [("acc", b_) for b_ in nat]
                                if g == 0:
                                    P.cp("dve", av, pv, [("pso", qs_)], ares)
                                else:
                                    P.tt("dve", av, av, pv, ALU.add, [("pso", qs_)] + ares, ares)
                            pending.append(pvtask)
                            flush(DLY)
                        prev = ks
                flush(0)
                if g == 2:
                    os_ = hd % 2
                    for i in range(ST // TT):
                        fs = fcount % 2
                        fcount += 1
                        ares = [("acc", b_) for b_ in range(i * 4, i * 4 + 4)]
                        P.recip(rs[fs][:], acc[:, 1, i * TT:(i + 1) * TT], ares, [("rs", fs)])
                        P.tt("dve", ost[os_][:, i * TT:(i + 1) * TT], acc[:, 0, i * TT:(i + 1) * TT], rs[fs][:], ALU.mult,
                             ares + [("rs", fs)], [("ost", os_, i)])
                    P.dma(self.zT[hd * 128:(hd + 1) * 128, st * ST:(st + 1) * ST], ost[os_][:],
                          [("ost", os_, i) for i in range(ST // TT)], [], ("osts", os_))
            P.emit()

    def sub_lru(self, layer, st, src, dst):
        j = layer // 2
        with ExitStack() as sub:
            h = self.sb(sub, [128, 8, ST], BF16, "h_lru%d_%d" % (layer, st))
            self.phase_norm(src, st, ("mix_pre", layer), h)
            self.phase_lru(j, st, h)
        self.phase_down(self.zT[0:D, :], 8, self.w_out[j], ("mix_post", layer), src, dst, st)

    def phase_lru(self, j, st, h):
        with ExitStack() as ph:
            wst = [self.sb(ph, [128, 8, 256], F32) for _ in range(2)]
            wgst = [self.sb(ph, [128, 2, 256], F32) for _ in range(2)]
            wx = [self.sb(ph, [128, 8, 256], BF16) for _ in range(2)]
            wg = [self.sb(ph, [128, 8, 256], BF16) for _ in range(2)]
            wga = [self.sb(ph, [128, 2, 256], BF16) for _ in range(2)]
            wgx = [self.sb(ph, [128, 2, 256], BF16) for _ in range(2)]
            NS = 3
            xr = [[self.sb(ph, [128, TT + 3], F32) for _ in range(NS)] for _ in range(2)]
            xc = [[self.sb(ph, [128, TT], F32) for _ in range(NS)] for _ in range(2)]
            xcb = [[self.sb(ph, [128, TT], BF16) for _ in range(NS)] for _ in range(2)]
            rt = [self.sb(ph, [128, TT], F32) for _ in range(2)]
            it = [self.sb(ph, [128, TT], F32) for _ in range(2)]
            at = [self.sb(ph, [128, TT], F32) for _ in range(2)]
            mt = [self.sb(ph, [128, TT], F32) for _ in range(2)]
            bt = [self.sb(ph, [128, TT], F32) for _ in range(2)]
            hs = [[self.sb(ph, [128, TT], F32) for _ in range(2)] for _ in range(2)]
            gt = [self.sb(ph, [128, TT], F32) for _ in range(2)]
            yt = [self.sb(ph, [128, TT], BF16) for _ in range(4)]
            P = Phase(self.S)
            if st == 0:
                P.memset("pool", self.xbtail[:], 0.0, ["xbtail"])
                P.memset("pool", self.hstate[:], 0.0, ["hstate"])
            win = self.w_in[j]
            ocw = CST_OFF[("lru_cw", j)]
            ocb = CST_OFF[("lru_cb", j)]
            ogab = CST_OFF[("lru_gab", j)]
            ogxb = CST_OFF[("lru_gxb", j)]
            ntile = ST // TT
            nw = [0]

            def load_block(n):
                s = n % 2
                for (col0, dstt, nm) in ((n * 256, wx[s], "wx"), (D + n * 256, wg[s], "wg")):
                    ss = nw[0] % 2
                    nw[0] += 1
                    self.load_w(P, win[:, col0:col0 + 256], wst[ss][:], ("wst", ss), dstt[:], (nm, s))
                for (wd, dstt, nm) in ((self.w_ga, wga[s], "wga"), (self.w_gx, wgx[s], "wgx")):
                    ss = nw[0] % 2
                    nw[0] += 1
                    self.load_w(P, wd[j, n], wgst[ss][:], ("wgst", ss), dstt[:], (nm, s))
            load_block(0)
            yc = [0]

            def stageA(n, i):
                ws = n % 2
                s = (n * ntile + i) % NS
                sp_ = (n * ntile + i - 1) % NS
                hk = [("h", i)]
                for cc in range(2):
                    c = 2 * n + cc
                    pb = self.ps[2 + cc]
                    kp = ("psx", cc)
                    kx = ("xr", cc, s)
                    kxc = ("xrc", cc, s)
                    for kc in range(8):
                        P.mm(pb[:], wx[ws][:, kc, cc * 128:(cc + 1) * 128], h[:, kc, i * TT:(i + 1) * TT],
                             kc == 0, kc == 7, [("wx", ws)] + hk, [kp])
                    P.cp("act", xr[cc][s][:, 3:TT + 3], pb[:], [kp], [kx])
                    if i == 0:
                        P.cp("act", xr[cc][s][:, 0:3], self.xbtail[:, c, :], ["xbtail", ("xbt", c)], [kxc])
                    else:
                        P.cp("act", xr[cc][s][:, 0:3], xr[cc][sp_][:, TT:TT + 3], [("xr", cc, sp_)], [kxc])
                    if i == ntile - 1:
                        P.cp("pool", self.xbtail[:, c, :], xr[cc][s][:, TT:TT + 3], [kx], [("xbt", c)])
                    wt = [self.cst[:, ocw + t * 8 + c:ocw + t * 8 + c + 1] for t in range(4)]
                    bb = self.cst[:, ocb + c:ocb + c + 1]
                    kcv = ("xc", cc, s)
                    P.ts("dve", xc[cc][s][:], xr[cc][s][:, 3:TT + 3], wt[3], bb, ALU.mult, ALU.add, [kx, "cst"], [kcv])
                    for t in (2, 1, 0):
                        P.stt(xc[cc][s][:], xr[cc][s][:, t:t + TT], wt[t], xc[cc][s][:], ALU.mult, ALU.add,
                              [kx, kxc, kcv], [kcv])
                    P.cp("act", xcb[cc][s][:], xc[cc][s][:], [kcv], [("xcb", cc, s)])

            def stageG(n, i):
                ws = n % 2
                for oc in range(2):
                    pg = self.ps[0 + oc]
                    kpg = ("psg", oc)
                    for kc in range(8):
                        P.mm(pg[:], wg[ws][:, kc, oc * 128:(oc + 1) * 128], h[:, kc, i * TT:(i + 1) * TT],
                             kc == 0, kc == 7, [("wg", ws), ("h", i)], [kpg])
                    P.act(gt[oc][:], pg[:], AF.Gelu_apprx_tanh, [kpg], [("gt", oc)])

            def stageB(n, i):
                ws = n % 2
                s = (n * ntile + i) % NS
                for oc in range(2):
                    pr = self.ps[4 + oc]
                    pi_ = self.ps[6 + oc]
                    for kc in range(2):
                        P.mm(pr[:], wga[ws][:, kc, oc * 128:(oc + 1) * 128], xcb[kc][s][:], kc == 0, kc == 1,
                             [("wga", ws), ("xcb", kc, s)], [("psr", oc)])
                    for kc in range(2):
                        P.mm(pi_[:], wgx[ws][:, kc, oc * 128:(oc + 1) * 128], xcb[kc][s][:], kc == 0, kc == 1,
                             [("wgx", ws), ("xcb", kc, s)], [("psi", oc)])
                for oc in range(2):
                    c = 2 * n + oc
                    q = oc
                    P.act(rt[q][:], self.ps[4 + oc][:], AF.Sigmoid, [("psr", oc), "cst"], [("rt", q)],
                          bias=self.cst[:, ogab + c:ogab + c + 1])
                    P.act(it[q][:], self.ps[6 + oc][:], AF.Sigmoid, [("psi", oc), "cst"], [("it", q)],
                          bias=self.cst[:, ogxb + c:ogxb + c + 1])
                    P.act(at[q][:], rt[q][:], AF.Exp, [("rt", q), ("ls8", j)], [("at", q)],
                          scale=self.ls8[:, j * 8 + c:j * 8 + c + 1])
                for oc in range(2):
                    q = oc
                    P.tt("dve", mt[q][:], at[q][:], at[q][:], ALU.mult, [("at", q)], [("mt", q)])
                    P.ts("dve", mt[q][:], mt[q][:], -1.0, 1.0, ALU.mult, ALU.add, [("mt", q)], [("mt", q)])
                for oc in range(2):
                    q = oc
                    P.act(mt[q][:], mt[q][:], AF.Sqrt, [("mt", q)], [("mt", q)], bias=self.tinyb[:])
                for oc in range(2):
                    c = 2 * n + oc
                    q = oc
                    P.tt("dve", bt[q][:], mt[q][:], it[q][:], ALU.mult, [("mt", q), ("it", q)], [("bt", q)])
                    P.tt("dve", bt[q][:], bt[q][:], xc[oc][s][:], ALU.mult, [("bt", q), ("xc", oc, s)], [("bt", q)])
                    hcur = hs[oc][i % 2]
                    hprev = hs[oc][(i - 1) % 2]
                    if i == 0:
                        init = self.hstate[:, c:c + 1]
                        ires = ["hstate", ("hst", c)]
                    else:
                        init = hprev[:, TT - 1:TT]
                        ires = [("hs", oc, (i - 1) % 2)]
                    P.add("dve", (lambda o_, a_, b_, i_: (lambda e: e.tensor_tensor_scan(o_, a_, b_, i_, ALU.mult, ALU.add)))(
                        hcur[:], at[q][:], bt[q][:], init), [("at", q), ("bt", q)] + ires, [("hs", oc, i % 2)])
                    if i == ntile - 1:
                        P.cp("pool", self.hstate[:, c:c + 1], hcur[:, TT - 1:TT], [("hs", oc, i % 2)], [("hst", c)])
                    ys = yc[0] % 4
                    yc[0] += 1
                    P.tt("dve", yt[ys][:], hcur[:], gt[oc][:], ALU.mult, [("hs", oc, i % 2), ("gt", oc)], [("yt", ys)])
                    P.dma(self.zT[c * 128:(c + 1) * 128, st * ST + i * TT: st * ST + (i + 1) * TT], yt[ys][:],
                          [("yt", ys)], [], ("yts", ys))

            for n in range(4):
                if n + 1 < 4:
                    load_block(n + 1)
                stageA(n, 0)
                for i in range(ntile):
                    if i + 1 < ntile:
                        stageA(n, i + 1)
                    stageG(n, i)
                    stageB(n, i)
            P.emit()


def build_program(S_tok, plan):
    nc = bass.Bass("TRN2", target_bir_lowering=False)
    b = Builder(nc, S_tok, plan)
    b.build()
    return nc


FULL_PLAN = []
for _l in range(DEPTH):
    FULL_PLAN.append(("attn" if _l % 2 == 0 else "lru", _l))
    FULL_PLAN.append(("ffn", _l))

W_NAMES = ["attn_w_qkv", "attn_w_o", "lru_w_in", "lru_ga_w", "lru_gx_w", "lru_w_out", "ffn_w_up", "ffn_w_down"]


def kernel(**inputs):
    x = np.asarray(inputs["x"], np.float32)
    B, S, _ = x.shape
    n_act = B
    nc = build_program(S, FULL_PLAN)
    cst = build_consts(inputs)
    bm = build_biasmask(inputs["rel_bias"])
    shared = {k: np.ascontiguousarray(np.asarray(inputs[k], np.float32)) for k in W_NAMES}
    in_maps = []
    for c in range(n_act):
        m = {"xT": np.ascontiguousarray(x[c].T), "cst": cst, "bm": bm}
        m.update(shared)
        in_maps.append(m)
    res = run_bass_kernel_spmd(nc, in_maps, core_ids=list(range(n_act)))
    out = np.stack([np.ascontiguousarray(res.results[c]["yT"].T) for c in range(n_act)], axis=0)
    return out.astype(np.float32)
```

```python
import numpy as np
import os
DBGD = os.environ.get('DBGD', '')
from contextlib import ExitStack
import concourse.bass as bass
import concourse.mybir as mybir
from concourse.bass_utils import run_bass_kernel_spmd

F32, BF16 = mybir.dt.float32, mybir.dt.bfloat16
AF = mybir.ActivationFunctionType
ALU = mybir.AluOpType

D = 1024
DEPTH = 4
DIL = (1, 4, 16)
DFF = 3072
ST = 4096
TT = 512
EPS = 1e-6
SCALE = 128 ** -0.5
N_CORES = 8

def _cst_layout():
    off = {}
    n = 0
    def add(name, k):
        nonlocal n
        off[name] = n
        n += k
    for kind in ("mix_pre", "mix_post", "ffn_pre", "ffn_post"):
        for l in range(DEPTH):
            add((kind, l), 8)
    for j in range(2):
        add(("lru_cw", j), 4 * 8)
        add(("lru_cb", j), 8)
        add(("lru_gab", j), 8)
        add(("lru_gxb", j), 8)
        add(("lru_lam", j), 8)
    for l in range(DEPTH):
        add(("ffn_cw", l), 3 * 48)
        add(("ffn_cb", l), 48)
    return off, n

CST_OFF, NCST = _cst_layout()


def _pm(v):
    return np.ascontiguousarray(np.asarray(v, np.float32).reshape(-1, 128).T)


def build_consts(inp):
    c = np.zeros((128, NCST), np.float32)
    names = {"mix_pre": "norm_mix_pre", "mix_post": "norm_mix_post",
             "ffn_pre": "norm_ffn_pre", "ffn_post": "norm_ffn_post"}
    for kind, nm in names.items():
        for l in range(DEPTH):
            o = CST_OFF[(kind, l)]
            c[:, o:o + 8] = _pm(inp[nm][l])
    for j in range(2):
        o = CST_OFF[("lru_cw", j)]
        for t in range(4):
            c[:, o + t * 8:o + t * 8 + 8] = _pm(inp["lru_conv_w"][j, t])
        c[:, CST_OFF[("lru_cb", j)]:CST_OFF[("lru_cb", j)] + 8] = _pm(inp["lru_conv_b"][j])
        c[:, CST_OFF[("lru_gab", j)]:CST_OFF[("lru_gab", j)] + 8] = _pm(inp["lru_ga_b"][j].reshape(-1))
        c[:, CST_OFF[("lru_gxb", j)]:CST_OFF[("lru_gxb", j)] + 8] = _pm(inp["lru_gx_b"][j].reshape(-1))
        c[:, CST_OFF[("lru_lam", j)]:CST_OFF[("lru_lam", j)] + 8] = _pm(inp["lru_lambda"][j])
    for l in range(DEPTH):
        o = CST_OFF[("ffn_cw", l)]
        for t in range(3):
            c[:, o + t * 48:o + t * 48 + 48] = _pm(inp["ffn_conv_w"][l, t])
        o = CST_OFF[("ffn_cb", l)]
        c[:, o:o + 48] = _pm(inp["ffn_conv_b"][l])
    return c


def _t5_bucket(dist):
    max_exact = 16
    d = np.maximum(dist, 1).astype(np.float64)
    large = max_exact + (np.log(d / max_exact) / np.log(2048 / max_exact) * (32 - max_exact)).astype(np.int32)
    large = np.minimum(large, 31)
    return np.where(dist < max_exact, dist, large).astype(np.int32)


def build_biasmask(rel_bias):
    rel_bias = np.asarray(rel_bias, np.float32)
    bm = np.full((128, 3, 8, 256), -1e30, np.float32)
    kk = np.arange(128)[:, None]
    ii = np.arange(128)[None, :]
    for g in range(3):
        d = DIL[g]
        m_same = ii - kk
        m_next = ii + 128 - kk
        for hd in range(8):
            tbl = rel_bias[:, g * 8 + hd]
            b_same = tbl[_t5_bucket(np.clip(m_same, 0, 128) * d)]
            b_next = tbl[_t5_bucket(np.clip(m_next, 0, 128) * d)]
            bm[:, g, hd, 0:128] = np.where(m_same >= 0, b_same, np.float32(-1e30))
            bm[:, g, hd, 128:256] = np.where(m_next <= 128, b_next, np.float32(-1e30))
    return np.ascontiguousarray(bm.reshape(128, 3 * 8 * 256))


ENGS = ("pe", "act", "dve", "pool", "sp")


class Sched:
    def __init__(self, nc, stack, n_dma=64):
        self.nc = nc
        self.esem = {e: stack.enter_context(nc.semaphore("se_" + e)) for e in ENGS}
        self.ecnt = {e: 0 for e in ENGS}
        self.dsem = [stack.enter_context(nc.semaphore("sd%d" % i)) for i in range(n_dma)]
        self.dcnt = [0] * n_dma
        self.nphase = 0


class Phase:
    def __init__(self, S):
        self.S = S
        self.ops = []
        self.lw = {}
        self.rd = {}
        self.keyidx = {}

    def add(self, eng, fn, reads=(), writes=(), dma_key=None):
        idx = len(self.ops)
        deps = set()
        for r in reads:
            w = self.lw.get(r)
            if w is not None:
                deps.add(w)
        for r in writes:
            w = self.lw.get(r)
            if w is not None:
                deps.add(w)
            rl = self.rd.get(r)
            if rl:
                deps.update(rl)
        for r in reads:
            self.rd.setdefault(r, []).append(idx)
        for r in writes:
            self.lw[r] = idx
            self.rd[r] = []
        deps.discard(idx)
        dk = None
        if dma_key is not None:
            if dma_key not in self.keyidx:
                self.keyidx[dma_key] = len(self.keyidx)
                assert len(self.keyidx) <= len(self.S.dsem), "too many dma keys"
            dk = self.keyidx[dma_key]
        self.ops.append([eng, fn, deps, dk, False, 0])
        return idx

    def mm(self, out, lhsT, rhs, start, stop, reads, writes):
        self.add("pe", lambda e: e.matmul(out, lhsT, rhs, start=start, stop=stop), reads, writes)

    def act(self, out, in_, func, reads, writes, scale=None, bias=None):
        kw = {}
        if scale is not None:
            kw["scale"] = scale
        if bias is not None:
            kw["bias"] = bias
        self.add("act", lambda e: e.activation(out=out, in_=in_, func=func, **kw), reads, writes)

    def ts(self, eng, out, in0, s1, s2, op0, op1, reads, writes):
        if op1 is None:
            self.add(eng, lambda e: e.tensor_scalar(out, in0, s1, None, op0), reads, writes)
        else:
            self.add(eng, lambda e: e.tensor_scalar(out, in0, s1, s2, op0, op1), reads, writes)

    def stt(self, out, in0, scalar, in1, op0, op1, reads, writes):
        self.add("dve", lambda e: e.scalar_tensor_tensor(out, in0, scalar, in1, op0, op1), reads, writes)

    def tt(self, eng, out, in0, in1, op, reads, writes):
        self.add(eng, lambda e: e.tensor_tensor(out, in0, in1, op), reads, writes)

    def cp(self, eng, out, in_, reads, writes):
        if eng == "act":
            self.add(eng, lambda e: e.copy(out, in_), reads, writes)
        else:
            self.add(eng, lambda e: e.tensor_copy(out, in_), reads, writes)

    def recip(self, out, in_, reads, writes):
        self.add("dve", lambda e: e.reciprocal(out, in_), reads, writes)

    def memset(self, eng, ap, val, writes):
        self.add(eng, lambda e: e.memset(ap, val), (), writes)

    def dma(self, out, in_, reads, writes, key, eng="sp"):
        self.add(eng, lambda e: e.dma_start(out=out, in_=in_), reads, writes, dma_key=key)

    def emit(self):
        S = self.S
        nc = S.nc
        ops = self.ops
        n = len(ops)
        dcount = {}
        dc = list(S.dcnt)
        for i, op in enumerate(ops):
            if op[3] is not None:
                dc[op[3]] += 1
                dcount[i] = dc[op[3]]
        need = [None] * n
        for i, op in enumerate(ops):
            eng = op[0]
            per_eng = {}
            per_dma = {}
            for y in op[2]:
                oy = ops[y]
                if oy[3] is not None:
                    k = oy[3]
                    if dcount[y] > per_dma.get(k, 0):
                        per_dma[k] = dcount[y]
                else:
                    if oy[0] == "pe" and eng == "pe":
                        continue
                    if y > per_eng.get(oy[0], -1):
                        per_eng[oy[0]] = y
            for y in per_eng.values():
                ops[y][4] = True
            need[i] = (per_eng, per_dma)
        ec = dict(S.ecnt)
        for op in ops:
            if op[3] is None and op[4]:
                ec[op[0]] += 1
                op[5] = ec[op[0]]
        per = {e: [] for e in ENGS}
        for i, op in enumerate(ops):
            per[op[0]].append(i)
        final_dma = [(k, dc[k]) for k in range(len(dc)) if dc[k] > S.dcnt[k]]

        def emit_eng(e, name):
            waited = {}
            for i in per[name]:
                op = ops[i]
                pe_, pd_ = need[i]
                for en, y in pe_.items():
                    v = ops[y][5]
                    key = ("e", en)
                    if waited.get(key, 0) < v:
                        e.wait_ge(S.esem[en], v)
                        waited[key] = v
                for k, c in pd_.items():
                    key = ("d", k)
                    if waited.get(key, 0) < c * 16:
                        e.wait_ge(S.dsem[k], c * 16)
                        waited[key] = c * 16
                ins = op[1](e)
                if op[3] is not None:
                    ins.then_inc(S.dsem[op[3]], 16)
                elif op[4]:
                    ins.then_inc(S.esem[name], 1)
            if name == "sp":
                for k, c in final_dma:
                    if waited.get(("d", k), 0) < c * 16:
                        e.wait_ge(S.dsem[k], c * 16)

        with nc.Block() as blk:
            @blk.tensor
            def _(e):
                emit_eng(e, "pe")

            @blk.scalar
            def _(e):
                emit_eng(e, "act")

            @blk.vector
            def _(e):
                emit_eng(e, "dve")

            @blk.gpsimd
            def _(e):
                emit_eng(e, "pool")

            @blk.sync
            def _(e):
                emit_eng(e, "sp")
        S.ecnt = ec
        S.dcnt = dc
        S.nphase += 1


class Builder:
    def __init__(self, nc, S_tok, plan):
        self.nc = nc
        self.S_tok = S_tok
        self.NT = S_tok // ST
        self.plan = plan
        self._n = 0

    def sb(self, stack, shape, dt, name=None):
        self._n += 1
        nm = name or ("t%d" % self._n)
        t = stack.enter_context(self.nc.sbuf_tensor(nm, shape, dt))
        return t

    def dram_in(self, name, shape, dt=F32):
        return self.nc.dram_tensor(name, list(shape), dt, kind="ExternalInput").ap()

    def build(self):
        nc = self.nc
        S_tok = self.S_tok
        self.x_in = self.dram_in("xT", [D, S_tok])
        self.y_out = nc.dram_tensor("yT", [D, S_tok], F32, kind="ExternalOutput").ap()
        self.cst_d = self.dram_in("cst", [128, NCST])
        self.bm_d = self.dram_in("bm", [128, 3 * 8 * 256])
        self.w_qkv = self.dram_in("attn_w_qkv", [2, D, 9216])
        self.w_o = self.dram_in("attn_w_o", [2, D, D])
        self.w_in = self.dram_in("lru_w_in", [2, D, 2 * D])
        self.w_ga = self.dram_in("lru_ga_w", [2, 4, 256, 256])
        self.w_gx = self.dram_in("lru_gx_w", [2, 4, 256, 256])
        self.w_out = self.dram_in("lru_w_out", [2, D, D])
        self.w_up = self.dram_in("ffn_w_up", [DEPTH, D, 2 * DFF])
        self.w_down = self.dram_in("ffn_w_down", [DEPTH, DFF, D])
        self.xs = nc.dram_tensor("xs", [D, S_tok], F32, kind="Internal").ap()
        self.zT = nc.dram_tensor("zT", [DFF, S_tok], BF16, kind="Internal").ap()
        self.QT = nc.dram_tensor("QT", [3, 8, 128, S_tok], BF16, kind="Internal").ap()
        self.KT = nc.dram_tensor("KT", [3, 8, 128, S_tok], BF16, kind="Internal").ap()
        self.V = nc.dram_tensor("Vs", [3, 8, 128, S_tok // 128, 128], BF16, kind="Internal").ap()

        with ExitStack() as top:
            self.S = Sched(nc, top)
            self.ps = [top.enter_context(nc.psum_tensor("ps%d" % i, [128, 512], F32)) for i in range(8)]
            self.cst = self.sb(top, [128, NCST], F32, "cst_sb")
            self.ones = self.sb(top, [128, 128], BF16, "ones")
            self.utail = self.sb(top, [128, 48, 2], F32, "utail")
            self.xbtail = self.sb(top, [128, 8, 3], F32, "xbtail")
            self.hstate = self.sb(top, [128, 8], F32, "hstate")
            self.ls8 = self.sb(top, [128, 16], F32, "ls8")
            self.epsb = self.sb(top, [128, 1], F32, "epsb")
            self.tinyb = self.sb(top, [128, 1], F32, "tinyb")
            self.phase_init()
            cur = self.x_in
            nsub = len(self.plan)
            for si, (kind, layer) in enumerate(self.plan):
                dst = self.y_out if si == nsub - 1 else self.xs
                for st in range(self.NT):
                    if kind == "attn":
                        self.sub_attn(layer, st, cur, dst)
                    elif kind == "lru":
                        self.sub_lru(layer, st, cur, dst)
                    elif kind == "ffn":
                        self.sub_ffn(layer, st, cur, dst)
                cur = dst
        return nc

    def ccol(self, key, i=0, n=1):
        o = CST_OFF[key] + i
        return self.cst[:, o:o + n]

    def phase_init(self):
        P = Phase(self.S)
        P.dma(self.cst[:], self.cst_d[:, :], (), ["cst"], "cst")
        P.memset("dve", self.ones[:], 1.0, ["ones"])
        P.memset("dve", self.epsb[:], EPS, ["epsb"])
        P.memset("dve", self.tinyb[:], 1e-30, ["tinyb"])
        for j in range(2):
            o = CST_OFF[("lru_lam", j)]
            dst = self.ls8[:, j * 8:(j + 1) * 8]
            P.act(dst, self.cst[:, o:o + 8], AF.Exp, ["cst"], [("ls8", j)], scale=-1.0)
            P.ts("dve", dst, dst, 1.0, None, ALU.add, None, [("ls8", j)], [("ls8", j)])
            P.act(dst, dst, AF.Ln, [("ls8", j)], [("ls8", j)])
            P.ts("dve", dst, dst, -8.0, None, ALU.mult, None, [("ls8", j)], [("ls8", j)])
        P.emit()

    def load_w(self, P, w_ap, stage, stage_key, dst_ap, dst_key):
        P.dma(stage, w_ap.rearrange("(kc p) c -> p kc c", p=128), (), [stage_key], stage_key)
        P.cp("pool", dst_ap, stage, [stage_key], [dst_key])

    def phase_norm(self, src, st, gkey, h):
        with ExitStack() as ph:
            xt = [self.sb(ph, [128, 8, TT], F32) for _ in range(2)]
            sq = [self.sb(ph, [128, 8, TT], BF16) for _ in range(2)]
            sd = [self.sb(ph, [128, TT], F32) for _ in range(2)]
            P = Phase(self.S)
            psn = [self.ps[0], self.ps[1]]
            for i in range(ST // TT):
                s = i % 2
                c0 = st * ST + i * TT
                kx, kq, kd, kp = ("nx", s), ("nq", s), ("nd", s), ("psn", s)
                P.dma(xt[s][:], src[:, c0:c0 + TT].rearrange("(kc p) t -> p kc t", p=128), (), [kx], kx)
                P.act(sq[s][:], xt[s][:], AF.Square, [kx], [kq])
                for kc in range(8):
                    P.mm(psn[s][:], self.ones[:], sq[s][:, kc, :], kc == 0, kc == 7, [kq, "ones"], [kp])
                P.act(sd[s][:], psn[s][:], AF.Sqrt, [kp], [kd], scale=1.0 / D, bias=self.epsb[:])
                P.recip(sd[s][:], sd[s][:], [kd], [kd])
                for kc in range(8):
                    P.stt(h[:, kc, i * TT:(i + 1) * TT], xt[s][:, kc, :], self.ccol(gkey, kc), sd[s][:],
                          ALU.mult, ALU.mult, [kx, kd, "cst"], [("h", i)])
            P.emit()

    def phase_down(self, z_d, KC, w_d, gkey, src, dst, st):
        with ExitStack() as ph:
            wres = self.sb(ph, [128, KC, D], BF16)
            wst = [self.sb(ph, [128, 4, D], F32) for _ in range(2)]
            zt = [self.sb(ph, [128, KC, TT], BF16) for _ in range(2)]
            xt = self.sb(ph, [128, 8, TT], F32)
            ot = self.sb(ph, [128, 8, TT], F32)
            sq = self.sb(ph, [128, 8, TT], BF16)
            sd = self.sb(ph, [128, TT], F32)
            P = Phase(self.S)
            for q in range(KC // 4):
                s = q % 2
                if 'w' in DBGD:
                    continue
                if 'W' in DBGD:
                    P.dma(wst[s][:], w_d[q * 512:(q + 1) * 512, :].rearrange("(kc p) c -> p kc c", p=128), (), [("wst", s)], ("wst", s))
                    continue
                self.load_w(P, w_d[q * 512:(q + 1) * 512, :], wst[s][:], ("wst", s),
                            wres[:, q * 4:(q + 1) * 4, :], ("wres", q))
            wkeys = [("wres", q) for q in range(KC // 4)]
            ntile = ST // TT

            def load_z(i):
                c0 = st * ST + i * TT
                for q8 in range(KC // 8):
                    P.dma(zt[i % 2][:, q8 * 8:(q8 + 1) * 8, :],
                          z_d[q8 * 1024:(q8 + 1) * 1024, c0:c0 + TT].rearrange("(kc p) t -> p kc t", p=128), (),
                          [("zt", i % 2, q8)], ("zt", i % 2, q8))
            if 'z' not in DBGD:
                load_z(0)
            for i in range(ntile):
                if i + 1 < ntile and 'z' not in DBGD:
                    load_z(i + 1)
                s = i % 2
                c0 = st * ST + i * TT
                P.dma(xt[:], src[:, c0:c0 + TT].rearrange("(kc p) t -> p kc t", p=128), (), ["dx"], "dx")
                lvl = int(os.environ.get('DBGL', '9'))
                for oc in range(8 if lvl >= 2 else 0):
                    pb = self.ps[2 + oc % 4]
                    kp = ("pso", oc % 4)
                    for kc in range(KC if 'm' not in DBGD else 1):
                        P.mm(pb[:], wres[:, kc, oc * 128:(oc + 1) * 128], zt[s][:, kc, :], kc == 0, kc == (KC - 1 if 'm' not in DBGD else 0),
                             [("zt", s, kc // 8)] + wkeys, [kp])
                    P.act(sq[:, oc, :], pb[:], AF.Square, [kp], [("dsq", oc)])
                    P.cp("dve", ot[:, oc, :], pb[:], [kp, ("dsq", oc)], [("dot", oc)])
                for kc in range(8 if lvl >= 3 else 0):
                    P.mm(self.ps[0][:], self.ones[:], sq[:, kc, :], kc == 0, kc == 7, [("dsq", kc), "ones"], ["psn"])
                if lvl >= 3:
                    P.act(sd[:], self.ps[0][:], AF.Sqrt, ["psn"], ["dsd"], scale=1.0 / D, bias=self.epsb[:])
                    P.recip(sd[:], sd[:], ["dsd"], ["dsd"])
                for oc in range(8 if lvl >= 4 else 0):
                    P.stt(ot[:, oc, :], ot[:, oc, :], self.ccol(gkey, oc), sd[:], ALU.mult, ALU.mult,
                          [("dot", oc), "dsd", "cst"], [("dot", oc)])
                    if lvl >= 5:
                        P.tt("dve", xt[:, oc, :], xt[:, oc, :], ot[:, oc, :], ALU.add, [("dot", oc), "dx"], [("dxo", oc)])
                P.dma(dst[:, c0:c0 + TT].rearrange("(kc p) t -> p kc t", p=128), xt[:],
                      [("dxo", oc) for oc in range(8)] + ["dx"], [], "dxs")
            P.emit()

    def sub_ffn(self, layer, st, src, dst):
        with ExitStack() as sub:
            h = self.sb(sub, [128, 8, ST], BF16, "h_ffn%d_%d" % (layer, st))
            self.phase_norm(src, st, ("ffn_pre", layer), h)
            self.phase_ffn_up(layer, st, h)
        self.phase_down(self.zT, 24, self.w_down[layer], ("ffn_post", layer), src, dst, st)

    def phase_ffn_up(self, layer, st, h):
        WB = 256
        with ExitStack() as ph:
            wst = [self.sb(ph, [128, 8, WB], F32) for _ in range(2)]
            wbf = [self.sb(ph, [128, 8, WB], BF16) for _ in range(4)]
            ug = [self.sb(ph, [128, TT + 2], F32) for _ in range(2)]
            uv = [self.sb(ph, [128, TT + 2], F32) for _ in range(2)]
            cg = [self.sb(ph, [128, TT], F32) for _ in range(2)]
            cv = [self.sb(ph, [128, TT], F32) for _ in range(2)]
            gl = [self.sb(ph, [128, TT], F32) for _ in range(2)]
            ctmp = [self.sb(ph, [128, TT], F32) for _ in range(2)]
            zst = [self.sb(ph, [128, ST], BF16) for _ in range(2)]
            P = Phase(self.S)
            if st == 0:
                P.memset("pool", self.utail[:], 0.0, ["utail"])
            wup = self.w_up[layer]
            ocw = CST_OFF[("ffn_cw", layer)]
            ocb = CST_OFF[("ffn_cb", layer)]
            ntile = ST // TT
            hk = [("h", i) for i in range(ntile)]
            nwst = [0]

            def load_pair(jg):
                for part in range(2):
                    s = nwst[0] % 2
                    nwst[0] += 1
                    slot = (jg % 2) * 2 + part
                    c0 = part * DFF + jg * WB
                    self.load_w(P, wup[:, c0:c0 + WB], wst[s][:], ("wst", s), wbf[slot][:], ("wbf", slot))
            load_pair(0)
            tcount = 0
            for jg in range(DFF // WB):
                if jg + 1 < DFF // WB:
                    load_pair(jg + 1)
                for jj in range(WB // 128):
                    j = jg * (WB // 128) + jj
                    zs = j % 2
                    for i in range(ntile):
                        s = tcount % 2
                        tcount += 1
                        info = []
                        for part, (ub, cb, name) in enumerate(((ug, cg, "g"), (uv, cv, "v"))):
                            slot = (jg % 2) * 2 + part
                            pb = self.ps[(2 if part == 0 else 4) + s]
                            kp = ("psu", part, s)
                            ku = ("u", part, s)
                            kuc = ("uc", part, s)
                            kc_ = ("c", part, s)
                            ch = part * 24 + j
                            for kc in range(8):
                                P.mm(pb[:], wbf[slot][:, kc, jj * 128:(jj + 1) * 128], h[:, kc, i * TT:(i + 1) * TT],
                                     kc == 0, kc == 7, [("wbf", slot), ("h", i)], [kp])
                            w0 = self.cst[:, ocw + 0 * 48 + ch:ocw + 0 * 48 + ch + 1]
                            w1 = self.cst[:, ocw + 1 * 48 + ch:ocw + 1 * 48 + ch + 1]
                            w2 = self.cst[:, ocw + 2 * 48 + ch:ocw + 2 * 48 + ch + 1]
                            bb = self.cst[:, ocb + ch:ocb + ch + 1]
                            P.cp("act", ub[s][:, 2:TT + 2], pb[:], [kp], [ku])
                            P.act(cb[s][:], pb[:], AF.Identity, [kp, "cst"], [kc_], scale=w2, bias=bb)
                            if i == 0:
                                P.cp("act", ub[s][:, 0:2], self.utail[:, ch, :], ["utail", ("utl", ch)], [kuc])
                            else:
                                P.cp("act", ub[s][:, 0:2], ub[1 - s][:, TT:TT + 2], [("u", part, 1 - s)], [kuc])
                            if i == ntile - 1:
                                P.cp("pool", self.utail[:, ch, :], ub[s][:, TT:TT + 2], [ku], [("utl", ch)])
                            info.append((ub, cb, ku, kuc, kc_, w0, w1))
                        for tap in (1, 0):
                            for (ub, cb, ku, kuc, kc_, w0, w1) in info:
                                wv_ = w1 if tap == 1 else w0
                                P.stt(cb[s][:], ub[s][:, tap:tap + TT], wv_, cb[s][:], ALU.mult, ALU.add, [ku, kuc, kc_, "cst"], [kc_])
                        P.act(gl[s][:], cg[s][:], AF.Gelu_apprx_tanh, [("c", 0, s)], [("gl", s)])
                        P.tt("dve", zst[zs][:, i * TT:(i + 1) * TT], gl[s][:], cv[s][:], ALU.mult,
                             [("gl", s), ("c", 1, s)], [("zst", zs, i)])
                    P.dma(self.zT[j * 128:(j + 1) * 128, st * ST:(st + 1) * ST], zst[zs][:],
                          [("zst", zs, i) for i in range(ntile)], [], ("zsts", zs))
            P.emit()

    def sub_attn(self, layer, st, src, dst):
        j = layer // 2
        with ExitStack() as sub:
            h = self.sb(sub, [128, 8, ST], BF16, "h_att%d_%d" % (layer, st))
            self.phase_norm(src, st, ("mix_pre", layer), h)
            self.phase_qkv(j, st, h)
        self.phase_attcore(st)
        self.phase_down(self.zT[0:D, :], 8, self.w_o[j], ("mix_post", layer), src, dst, st)

    def hperm(self, h, kc, g, pos, n):
        d = DIL[g]
        U = ST // d
        if d == 1:
            return h[:, kc, pos:pos + n]
        hv = h[:, kc, :].rearrange("p (u r) -> p r u", r=d)
        r0, u0 = pos // U, pos % U
        if u0 + n <= U:
            return hv[:, r0, u0:u0 + n]
        assert u0 == 0 and n % U == 0
        return hv[:, r0:r0 + n // U, :]

    def phase_qkv(self, j, st, h):
        WB = 256
        with ExitStack() as ph:
            wst = [self.sb(ph, [128, 8, 512], F32) for _ in range(2)]
            wbf = [self.sb(ph, [128, 8, WB], BF16) for _ in range(2)]
            wv = [self.sb(ph, [128, 8, 512], BF16) for _ in range(2)]
            qst = [self.sb(ph, [128, ST], BF16) for _ in range(2)]
            vst = [self.sb(ph, [128, 8, 8, 128], BF16) for _ in range(2)]
            P = Phase(self.S)
            wq = self.w_qkv[j]
            ntile = ST // TT
            nw = [0]

            def ldw(col0, ncol, dst, dkey):
                s = nw[0] % 2
                nw[0] += 1
                self.load_w(P, wq[:, col0:col0 + ncol], wst[s][:, :, 0:ncol], ("wst", s), dst, dkey)
            hcount = 0
            ecount = 0
            hall = [("h", t) for t in range(ntile)]
            for g in range(3):
                d = DIL[g]
                U = ST // d
                blocks = [(jq, hb) for jq in range(2) for hb in range(4)]
                ldw(((g * 3 + 0) * 8 + 0) * 128, WB, wbf[0][:], ("wbf", 0))
                for bi, (jq, hb) in enumerate(blocks):
                    slot = bi % 2
                    if bi + 1 < len(blocks):
                        jq2, hb2 = blocks[bi + 1]
                        ldw(((g * 3 + jq2) * 8 + hb2 * 2) * 128, WB, wbf[(bi + 1) % 2][:], ("wbf", (bi + 1) % 2))
                    elif g + 1 <= 2:
                        pass
                    for hh in range(2):
                        hd = hb * 2 + hh
                        qs = hcount % 2
                        hcount += 1
                        for i in range(ntile):
                            pb = self.ps[2 + ecount % 4]
                            kp = ("psq", ecount % 4)
                            for kc in range(8):
                                P.mm(pb[:], wbf[slot][:, kc, hh * 128:(hh + 1) * 128], h[:, kc, i * TT:(i + 1) * TT],
                                     kc == 0, kc == 7, [("wbf", slot), ("h", i)], [kp])
                            eng = "act" if ecount % 2 == 0 else "dve"
                            if d == 1:
                                P.cp(eng, qst[qs][:, i * TT:(i + 1) * TT], pb[:], [kp], [("qst", qs, i)])
                            else:
                                w_ = TT // d
                                outv = qst[qs][:].rearrange("p (r u) -> p r u", r=d)[:, :, i * w_:(i + 1) * w_]
                                inv = pb[:].rearrange("p (u r) -> p r u", r=d)
                                P.cp(eng, outv, inv, [kp], [("qst", qs, i)])
                            ecount += 1
                        dstT = self.QT if jq == 0 else self.KT
                        P.dma(dstT[g, hd, :, st * ST:(st + 1) * ST], qst[qs][:], [("qst", qs, i) for i in range(ntile)], [], ("qsts", qs))
                for half in range(2):
                    ldw(((g * 3 + 2) * 8 + half * 4) * 128, 512, wv[half][:], ("wv", half))
                nblk = ST // 128
                for b in range(nblk):
                    vs = (b // 8) % 2
                    for half in range(2):
                        pb = self.ps[2 + ecount % 4]
                        kp = ("psq", ecount % 4)
                        for kc in range(8):
                            lhsT = self.hperm(h, kc, g, b * 128, 128)
                            P.mm(pb[:], lhsT, wv[half][:, kc, :], kc == 0, kc == 7, [("wv", half)] + hall, [kp])
                        eng = "act" if ecount % 2 == 0 else "dve"
                        P.cp(eng, vst[vs][:, half * 4:(half + 1) * 4, b % 8, :],
                             pb[:].rearrange("p (a b) -> p a b", a=4), [kp], [("vst", vs, b % 8, half)])
                        ecount += 1
                    if b % 8 == 7:
                        b0 = st * nblk + b - 7
                        P.dma(self.V[g, :, :, b0:b0 + 8, :].rearrange("h p b d -> p h b d"), vst[vs][:],
                              [("vst", vs, bb, hf) for bb in range(8) for hf in range(2)], [], ("vsts", vs))
            P.emit()

    def phase_attcore(self, st):
        with ExitStack() as ph:
            qt = [self.sb(ph, [128, ST], BF16) for _ in range(2)]
            kt = [self.sb(ph, [128, ST], BF16) for _ in range(2)]
            kh = [self.sb(ph, [128, 16, 128], BF16) for _ in range(2)]
            vt = [self.sb(ph, [128, ST // 128, 128], BF16) for _ in range(2)]
            vh = [self.sb(ph, [128, 16, 128], BF16) for _ in range(2)]
            acc = self.sb(ph, [128, 2, ST], F32)
            E32 = self.sb(ph, [128, 3, 8, 256], F32)
            E = self.sb(ph, [128, 3, 8, 256], BF16)
            ex = [self.sb(ph, [128, 256], BF16) for _ in range(4)]
            pT = [self.sb(ph, [128, 256], BF16) for _ in range(6)]
            ost = [self.sb(ph, [128, ST], BF16) for _ in range(2)]
            rs = [self.sb(ph, [128, TT], F32) for _ in range(2)]
            P = Phase(self.S)
            Ef32 = E32[:].rearrange("p a b c -> p (a b c)")
            Ef = E[:].rearrange("p a b c -> p (a b c)")
            P.dma(Ef32, self.bm_d[:, :], (), ["E32"], "E32")
            P.act(Ef, Ef32, AF.Exp, ["E32"], ["E"])
            units = [(hd, g) for hd in range(8) for g in range(3)]
            nblk = ST // 128

            def load_unit(ui):
                hd, g = units[ui]
                s = ui % 2
                d = DIL[g]
                U = ST // d
                nb = U // 128
                P.dma(qt[s][:], self.QT[g, hd, :, st * ST:(st + 1) * ST], (), [("qt", s)], ("qt", s))
                P.dma(kt[s][:], self.KT[g, hd, :, st * ST:(st + 1) * ST], (), [("kt", s)], ("kt", s))
                P.dma(vt[s][:], self.V[g, hd, :, st * nblk:(st + 1) * nblk, :], (), [("vt", s)], ("vt", s))
                if st > 0:
                    src = self.KT[g, hd, :, (st - 1) * ST:st * ST].rearrange("p (r u) -> p r u", r=d)[:, :, U - 128:U]
                    if d == 16:
                        P.dma(kh[s][:, 0:8, :], src[:, 0:8, :], (), [("kh", s, 0)], ("kh", s, 0))
                        P.dma(kh[s][:, 8:16, :], src[:, 8:16, :], (), [("kh", s, 1)], ("kh", s, 1))
                    else:
                        P.dma(kh[s][:, 0:d, :], src, (), [("kh", s, 0)], ("kh", s, 0))
                    srcv = self.V[g, hd, :, (st - 1) * nblk:st * nblk, :].rearrange("p (r n) e -> p r n e", r=d)[:, :, nb - 1, :]
                    if d == 16:
                        P.dma(vh[s][:, 0:8, :], srcv[:, 0:8, :], (), [("vh", s, 0)], ("vh", s, 0))
                        P.dma(vh[s][:, 8:16, :], srcv[:, 8:16, :], (), [("vh", s, 1)], ("vh", s, 1))
                    else:
                        P.dma(vh[s][:, 0:d, :], srcv, (), [("vh", s, 0)], ("vh", s, 0))
            load_unit(0)
            kcount = 0
            qcount = 0
            fcount = 0
            pending = []
            DLY = 2

            def flush(keep):
                while len(pending) > keep:
                    pending.pop(0)()
            for ui, (hd, g) in enumerate(units):
                if ui + 1 < len(units):
                    load_unit(ui + 1)
                s = ui % 2
                d = DIL[g]
                U = ST // d
                nb = U // 128
                hk_ = [0, 1] if d == 16 else [0]
                inres = [("qt", s), ("kt", s)] + ([("kh", s, q_) for q_ in hk_] if st > 0 else [])
                vres = [("vt", s)] + ([("vh", s, q_) for q_ in hk_] if st > 0 else [])
                for r in range(d):
                    prev = None
                    for n in range(-1 if st > 0 else 0, nb):
                        lo = n >= 0
                        hi = n + 1 < nb
                        kblk = kt[s][:, r * U + n * 128: r * U + (n + 1) * 128] if lo else kh[s][:, r, :]
                        qstart = r * U + (n if lo else n + 1) * 128
                        N = 128 * (int(lo) + int(hi))
                        c0 = 0 if lo else 128
                        ks4 = kcount % 4
                        ks = kcount % 6
                        es = kcount % 4
                        kcount += 1
                        pb = self.ps[ks4]
                        P.mm(pb[:, c0:c0 + N], kblk, qt[s][:, qstart:qstart + N], True, True, inres, [("pss", ks4)])
                        P.act(ex[es][:, c0:c0 + N], pb[:, c0:c0 + N], AF.Exp, [("pss", ks4)], [("ex", es)], scale=SCALE)
                        P.tt("dve", pT[ks][:, c0:c0 + N], ex[es][:, c0:c0 + N], E[:, g, hd, c0:c0 + N], ALU.mult,
                             [("ex", es), "E"], [("pT", ks)])
                        if lo:
                            m = n
                            parts = []
                            if prev is not None:
                                vprev = vt[s][:, r * nb + m - 1, :] if m - 1 >= 0 else vh[s][:, r, :]
                                parts.append((vprev, pT[prev][:, 128:256], ("pT", prev)))
                            parts.append((vt[s][:, r * nb + m, :], pT[ks][:, 0:128], ("pT", ks)))
                            def pvtask(parts=parts, m=m, r=r, d=d, g=g, vres=vres):
                                nonlocal qcount
                                qs_ = qcount % 4
                                qcount += 1
                                po = self.ps[4 + qs_]
                                for pi, (vb, pp, pk) in enumerate(parts):
                                    P.mm(po[:, 0:128], vb, pp, pi == 0, pi == len(parts) - 1, vres + [pk], [("pso", qs_)])
                                for pi, (vb, pp, pk) in enumerate(parts):
                                    P.mm(po[:, 128:256], self.ones[:], pp, pi == 0, pi == len(parts) - 1, ["ones", pk], [("pso", qs_)])
                                t0 = r + m * 128 * d
                                av = acc[:, :, t0:t0 + 127 * d + 1:d] if d > 1 else acc[:, :, t0:t0 + 128]
                                pv = po[:, 0:256].rearrange("p (a b) -> p a b", a=2)
                                nat = sorted(set(range(t0 // 128, (t0 + 128 * d - 1) // 128 + 1)))
                                ares = [("acc", b_) for b_ in nat]
                                if g == 0:
                                    P.cp("dve", av, pv, [("pso", qs_)], ares)
                                else:
                                    P.tt("dve", av, av, pv, ALU.add, [("pso", qs_)] + ares, ares)
                            pending.append(pvtask)
                            flush(DLY)
                        prev = ks
                flush(0)
                if g == 2:
                    os_ = hd % 2
                    for i in range(ST // TT):
                        fs = fcount % 2
                        fcount += 1
                        ares = [("acc", b_) for b_ in range(i * 4, i * 4 + 4)]
                        P.recip(rs[fs][:], acc[:, 1, i * TT:(i + 1) * TT], ares, [("rs", fs)])
                        P.tt("dve", ost[os_][:, i * TT:(i + 1) * TT], acc[:, 0, i * TT:(i + 1) * TT], rs[fs][:], ALU.mult,
                             ares + [("rs", fs)], [("ost", os_, i)])
                    P.dma(self.zT[hd * 128:(hd + 1) * 128, st * ST:(st + 1) * ST], ost[os_][:],
                          [("ost", os_, i) for i in range(ST // TT)], [], ("osts", os_))
            P.emit()

    def sub_lru(self, layer, st, src, dst):
        j = layer // 2
        with ExitStack() as sub:
            h = self.sb(sub, [128, 8, ST], BF16, "h_lru%d_%d" % (layer, st))
            self.phase_norm(src, st, ("mix_pre", layer), h)
            self.phase_lru(j, st, h)
        self.phase_down(self.zT[0:D, :], 8, self.w_out[j], ("mix_post", layer), src, dst, st)

    def phase_lru(self, j, st, h):
        with ExitStack() as ph:
            wst = [self.sb(ph, [128, 8, 256], F32) for _ in range(2)]
            wgst = [self.sb(ph, [128, 2, 256], F32) for _ in range(2)]
            wx = [self.sb(ph, [128, 8, 256], BF16) for _ in range(2)]
            wg = [self.sb(ph, [128, 8, 256], BF16) for _ in range(2)]
            wga = [self.sb(ph, [128, 2, 256], BF16) for _ in range(2)]
            wgx = [self.sb(ph, [128, 2, 256], BF16) for _ in range(2)]
            NS = 3
            xr = [[self.sb(ph, [128, TT + 3], F32) for _ in range(NS)] for _ in range(2)]
            xc = [[self.sb(ph, [128, TT], F32) for _ in range(NS)] for _ in range(2)]
            xcb = [[self.sb(ph, [128, TT], BF16) for _ in range(NS)] for _ in range(2)]
            rt = [self.sb(ph, [128, TT], F32) for _ in range(2)]
            it = [self.sb(ph, [128, TT], F32) for _ in range(2)]
            at = [self.sb(ph, [128, TT], F32) for _ in range(2)]
            mt = [self.sb(ph, [128, TT], F32) for _ in range(2)]
            bt = [self.sb(ph, [128, TT], F32) for _ in range(2)]
            hs = [[self.sb(ph, [128, TT], F32) for _ in range(2)] for _ in range(2)]
            gt = [self.sb(ph, [128, TT], F32) for _ in range(2)]
            yt = [self.sb(ph, [128, TT], BF16) for _ in range(4)]
            P = Phase(self.S)
            if st == 0:
                P.memset("pool", self.xbtail[:], 0.0, ["xbtail"])
                P.memset("pool", self.hstate[:], 0.0, ["hstate"])
            win = self.w_in[j]
            ocw = CST_OFF[("lru_cw", j)]
            ocb = CST_OFF[("lru_cb", j)]
            ogab = CST_OFF[("lru_gab", j)]
            ogxb = CST_OFF[("lru_gxb", j)]
            ntile = ST // TT
            nw = [0]

            def load_block(n):
                s = n % 2
                for (col0, dstt, nm) in ((n * 256, wx[s], "wx"), (D + n * 256, wg[s], "wg")):
                    ss = nw[0] % 2
                    nw[0] += 1
                    self.load_w(P, win[:, col0:col0 + 256], wst[ss][:], ("wst", ss), dstt[:], (nm, s))
                for (wd, dstt, nm) in ((self.w_ga, wga[s], "wga"), (self.w_gx, wgx[s], "wgx")):
                    ss = nw[0] % 2
                    nw[0] += 1
                    self.load_w(P, wd[j, n], wgst[ss][:], ("wgst", ss), dstt[:], (nm, s))
            load_block(0)
            yc = [0]

            def stageA(n, i):
                ws = n % 2
                s = (n * ntile + i) % NS
                sp_ = (n * ntile + i - 1) % NS
                hk = [("h", i)]
                for cc in range(2):
                    c = 2 * n + cc
                    pb = self.ps[2 + cc]
                    kp = ("psx", cc)
                    kx = ("xr", cc, s)
                    kxc = ("xrc", cc, s)
                    for kc in range(8):
                        P.mm(pb[:], wx[ws][:, kc, cc * 128:(cc + 1) * 128], h[:, kc, i * TT:(i + 1) * TT],
                             kc == 0, kc == 7, [("wx", ws)] + hk, [kp])
                    P.cp("act", xr[cc][s][:, 3:TT + 3], pb[:], [kp], [kx])
                    if i == 0:
                        P.cp("act", xr[cc][s][:, 0:3], self.xbtail[:, c, :], ["xbtail", ("xbt", c)], [kxc])
                    else:
                        P.cp("act", xr[cc][s][:, 0:3], xr[cc][sp_][:, TT:TT + 3], [("xr", cc, sp_)], [kxc])
                    if i == ntile - 1:
                        P.cp("pool", self.xbtail[:, c, :], xr[cc][s][:, TT:TT + 3], [kx], [("xbt", c)])
                    wt = [self.cst[:, ocw + t * 8 + c:ocw + t * 8 + c + 1] for t in range(4)]
                    bb = self.cst[:, ocb + c:ocb + c + 1]
                    kcv = ("xc", cc, s)
                    P.ts("dve", xc[cc][s][:], xr[cc][s][:, 3:TT + 3], wt[3], bb, ALU.mult, ALU.add, [kx, "cst"], [kcv])
                    for t in (2, 1, 0):
                        P.stt(xc[cc][s][:], xr[cc][s][:, t:t + TT], wt[t], xc[cc][s][:], ALU.mult, ALU.add,
                              [kx, kxc, kcv], [kcv])
                    P.cp("act", xcb[cc][s][:], xc[cc][s][:], [kcv], [("xcb", cc, s)])

            def stageG(n, i):
                ws = n % 2
                for oc in range(2):
                    pg = self.ps[0 + oc]
                    kpg = ("psg", oc)
                    for kc in range(8):
                        P.mm(pg[:], wg[ws][:, kc, oc * 128:(oc + 1) * 128], h[:, kc, i * TT:(i + 1) * TT],
                             kc == 0, kc == 7, [("wg", ws), ("h", i)], [kpg])
                    P.act(gt[oc][:], pg[:], AF.Gelu_apprx_tanh, [kpg], [("gt", oc)])

            def stageB(n, i):
                ws = n % 2
                s = (n * ntile + i) % NS
                for oc in range(2):
                    pr = self.ps[4 + oc]
                    pi_ = self.ps[6 + oc]
                    for kc in range(2):
                        P.mm(pr[:], wga[ws][:, kc, oc * 128:(oc + 1) * 128], xcb[kc][s][:], kc == 0, kc == 1,
                             [("wga", ws), ("xcb", kc, s)], [("psr", oc)])
                    for kc in range(2):
                        P.mm(pi_[:], wgx[ws][:, kc, oc * 128:(oc + 1) * 128], xcb[kc][s][:], kc == 0, kc == 1,
                             [("wgx", ws), ("xcb", kc, s)], [("psi", oc)])
                for oc in range(2):
                    c = 2 * n + oc
                    q = oc
                    P.act(rt[q][:], self.ps[4 + oc][:], AF.Sigmoid, [("psr", oc), "cst"], [("rt", q)],
                          bias=self.cst[:, ogab + c:ogab + c + 1])
                    P.act(it[q][:], self.ps[6 + oc][:], AF.Sigmoid, [("psi", oc), "cst"], [("it", q)],
                          bias=self.cst[:, ogxb + c:ogxb + c + 1])
                    P.act(at[q][:], rt[q][:], AF.Exp, [("rt", q), ("ls8", j)], [("at", q)],
                          scale=self.ls8[:, j * 8 + c:j * 8 + c + 1])
                for oc in range(2):
                    q = oc
                    P.tt("dve", mt[q][:], at[q][:], at[q][:], ALU.mult, [("at", q)], [("mt", q)])
                    P.ts("dve", mt[q][:], mt[q][:], -1.0, 1.0, ALU.mult, ALU.add, [("mt", q)], [("mt", q)])
                for oc in range(2):
                    q = oc
                    P.act(mt[q][:], mt[q][:], AF.Sqrt, [("mt", q)], [("mt", q)], bias=self.tinyb[:])
                for oc in range(2):
                    c = 2 * n + oc
                    q = oc
                    P.tt("dve", bt[q][:], mt[q][:], it[q][:], ALU.mult, [("mt", q), ("it", q)], [("bt", q)])
                    P.tt("dve", bt[q][:], bt[q][:], xc[oc][s][:], ALU.mult, [("bt", q), ("xc", oc, s)], [("bt", q)])
                    hcur = hs[oc][i % 2]
                    hprev = hs[oc][(i - 1) % 2]
                    if i == 0:
                        init = self.hstate[:, c:c + 1]
                        ires = ["hstate", ("hst", c)]
                    else:
                        init = hprev[:, TT - 1:TT]
                        ires = [("hs", oc, (i - 1) % 2)]
                    P.add("dve", (lambda o_, a_, b_, i_: (lambda e: e.tensor_tensor_scan(o_, a_, b_, i_, ALU.mult, ALU.add)))(
                        hcur[:], at[q][:], bt[q][:], init), [("at", q), ("bt", q)] + ires, [("hs", oc, i % 2)])
                    if i == ntile - 1:
                        P.cp("pool", self.hstate[:, c:c + 1], hcur[:, TT - 1:TT], [("hs", oc, i % 2)], [("hst", c)])
                    ys = yc[0] % 4
                    yc[0] += 1
                    P.tt("dve", yt[ys][:], hcur[:], gt[oc][:], ALU.mult, [("hs", oc, i % 2), ("gt", oc)], [("yt", ys)])
                    P.dma(self.zT[c * 128:(c + 1) * 128, st * ST + i * TT: st * ST + (i + 1) * TT], yt[ys][:],
                          [("yt", ys)], [], ("yts", ys))

            for n in range(4):
                if n + 1 < 4:
                    load_block(n + 1)
                stageA(n, 0)
                for i in range(ntile):
                    if i + 1 < ntile:
                        stageA(n, i + 1)
                    stageG(n, i)
                    stageB(n, i)
            P.emit()


def build_program(S_tok, plan):
    nc = bass.Bass("TRN2", target_bir_lowering=False)
    b = Builder(nc, S_tok, plan)
    b.build()
    return nc


FULL_PLAN = []
for _l in range(DEPTH):
    FULL_PLAN.append(("attn" if _l % 2 == 0 else "lru", _l))
    FULL_PLAN.append(("ffn", _l))

W_NAMES = ["attn_w_qkv", "attn_w_o", "lru_w_in", "lru_ga_w", "lru_gx_w", "lru_w_out", "ffn_w_up", "ffn_w_down"]


def kernel(**inputs):
    x = np.asarray(inputs["x"], np.float32)
    B, S, _ = x.shape
    n_act = B
    nc = build_program(S, FULL_PLAN)
    cst = build_consts(inputs)
    bm = build_biasmask(inputs["rel_bias"])
    shared = {k: np.ascontiguousarray(np.asarray(inputs[k], np.float32)) for k in W_NAMES}
    in_maps = []
    for c in range(n_act):
        m = {"xT": np.ascontiguousarray(x[c].T), "cst": cst, "bm": bm}
        m.update(shared)
        in_maps.append(m)
    res = run_bass_kernel_spmd(nc, in_maps, core_ids=list(range(n_act)))
    out = np.stack([np.ascontiguousarray(res.results[c]["yT"].T) for c in range(n_act)], axis=0)
    return out.astype(np.float32)
```
